# Optimizing a Trainium2 kernel written in Bass

```python
import math
import jax, jax.numpy as jnp
from jax import lax
import numpy as np

D_MODEL = 2048
BATCH = 4
SEQ = 4096
DEPTH = 4

HEAD_DIM = 128
N_QK_HEADS_A = 8
N_V_HEADS_A = 16
QK_WIDTH_A = N_QK_HEADS_A * HEAD_DIM
V_WIDTH_A = N_V_HEADS_A * HEAD_DIM
CONV_A = 4
CHUNK_A = 64
N_GROUPS_B = 8
GROUP_DIM_B = 128
WIDTH_B = N_GROUPS_B * GROUP_DIM_B
CHUNK_B = 128
N_BRANCH = 2
D_FF = 5632
CONV_FFN = 3
PLE_DIM = 256
EPS = 1e-6

SPLIT_SIZES = (QK_WIDTH_A, QK_WIDTH_A, V_WIDTH_A, V_WIDTH_A, N_V_HEADS_A, N_V_HEADS_A,
               WIDTH_B, WIDTH_B, N_BRANCH * D_MODEL)
N_IN = sum(SPLIT_SIZES)

kernel_name = "hybrid_deltanet_sgu_convglu_ple"


def rms_norm(x, gain):
    xf = x.astype(jnp.float32)
    y = xf * lax.rsqrt(jnp.mean(xf * xf, axis=-1, keepdims=True) + EPS)
    return (y * gain.astype(jnp.float32)).astype(x.dtype)


def l2_normalize(x):
    return x * lax.rsqrt(jnp.sum(x * x, axis=-1, keepdims=True) + EPS)


def causal_depthwise_conv(x, w):
    k = w.shape[0]
    return lax.conv_general_dilated(
        x, w[:, None, :].astype(x.dtype), window_strides=(1,), padding=((k - 1, 0),),
        dimension_numbers=("NWC", "WIO", "NWC"), feature_group_count=x.shape[-1])


def gated_delta_rule(q, k, v, beta, g):
    b, s, h, dk = q.shape
    dv = v.shape[-1]
    n = s // CHUNK_A

    def chunks(t):
        t = t.reshape((b, n, CHUNK_A, h) + t.shape[3:])
        return jnp.moveaxis(t, (1, 3), (0, 2))

    qc, kc, vc = chunks(q), chunks(k), chunks(v)
    bc, gc = chunks(beta), chunks(g)
    gam = jnp.cumsum(gc, axis=-1)
    causal = jnp.tril(jnp.ones((CHUNK_A, CHUNK_A), dtype=bool))
    strict = jnp.tril(jnp.ones((CHUNK_A, CHUNK_A), dtype=bool), -1)
    decay = jnp.exp(jnp.where(causal, gam[..., :, None] - gam[..., None, :], -jnp.inf))
    kb = kc * bc[..., None]
    a_mat = jnp.where(strict, jnp.einsum("nbhcd,nbhsd->nbhcs", kb, kc) * decay, 0.0)
    eye = jnp.eye(CHUNK_A, dtype=jnp.float32)
    rhs = jnp.concatenate([vc * bc[..., None], kb * jnp.exp(gam)[..., None]], axis=-1)
    sol = lax.linalg.triangular_solve(a_mat + eye, rhs, left_side=True, lower=True,
                                      unit_diagonal=True)
    u, w = sol[..., :dv], sol[..., dv:]
    attn = jnp.einsum("nbhcd,nbhsd->nbhcs", qc, kc) * decay
    q_dec = qc * jnp.exp(gam)[..., None]
    k_dec = kc * jnp.exp(gam[..., -1:] - gam)[..., None]
    chunk_decay = jnp.exp(gam[..., -1])

    def step(state, inp):
        q_i, k_i, u_i, w_i, a_i, d_i = inp
        v_new = u_i - jnp.einsum("bhcd,bhde->bhce", w_i, state)
        o = jnp.einsum("bhcd,bhde->bhce", q_i, state) + jnp.einsum("bhcs,bhse->bhce", a_i, v_new)
        state = state * d_i[..., None, None] + jnp.einsum("bhcd,bhce->bhde", k_i, v_new)
        return state, o

    s0 = jnp.zeros((b, h, dk, dv), jnp.float32)
    _, o = lax.scan(step, s0, (q_dec, k_dec, u, w, attn, chunk_decay))
    return jnp.moveaxis(o, (0, 2), (1, 3)).reshape(b, s, h, dv)


def delta_mixer(q, k, v, z, beta_logit, a_logit, conv_w, a_log, dt_bias, head_gain):
    b, s, _ = q.shape
    f32 = jnp.float32
    qkv = jax.nn.silu(causal_depthwise_conv(jnp.concatenate([q, k, v], axis=-1), conv_w))
    q, k, v = jnp.split(qkv, [QK_WIDTH_A, 2 * QK_WIDTH_A], axis=-1)
    rep = N_V_HEADS_A // N_QK_HEADS_A
    q = l2_normalize(q.reshape(b, s, N_QK_HEADS_A, HEAD_DIM).astype(f32)) * (HEAD_DIM ** -0.5)
    k = l2_normalize(k.reshape(b, s, N_QK_HEADS_A, HEAD_DIM).astype(f32))
    q = jnp.repeat(q, rep, axis=2)
    k = jnp.repeat(k, rep, axis=2)
    v = v.reshape(b, s, N_V_HEADS_A, HEAD_DIM).astype(f32)
    beta = jax.nn.sigmoid(beta_logit.astype(f32))
    g = -jnp.exp(a_log.astype(f32)) * jax.nn.softplus(a_logit.astype(f32) + dt_bias.astype(f32))
    o = gated_delta_rule(q, k, v, beta, g)
    o = rms_norm(o, head_gain) * jax.nn.silu(z.reshape(b, s, N_V_HEADS_A, HEAD_DIM).astype(f32))
    return o.reshape(b, s, V_WIDTH_A).astype(z.dtype)


def spatial_gating_mixer(u, v, norm_gain, w_s, b_s):
    b, s, _ = u.shape
    n = s // CHUNK_B
    u = jax.nn.gelu(u)
    v = rms_norm(jax.nn.gelu(v), norm_gain)
    causal = jnp.tril(jnp.ones((CHUNK_B, CHUNK_B), dtype=bool))
    w = jnp.where(causal, w_s, 0.0).astype(v.dtype)
    vc = v.reshape(b, n, CHUNK_B, N_GROUPS_B, GROUP_DIM_B)
    mixed = jnp.einsum("gts,bnsgc->bntgc", w, vc) + b_s.T[None, None, :, :, None].astype(v.dtype)
    return u * mixed.reshape(b, s, WIDTH_B)


def conv_glu_ffn(h, w_up, conv_w, conv_b, w_down):
    up = causal_depthwise_conv(h @ w_up, conv_w) + conv_b
    gate, val = jnp.split(up, 2, axis=-1)
    return (jax.nn.silu(gate) * val) @ w_down


def setup_inputs(seed: int = 0) -> dict:
    key = jax.random.key(seed)
    ks = jax.random.split(key, 24)
    f32 = jnp.float32

    def nrm(k, shape, scale):
        return jax.random.normal(k, shape, f32) * scale

    def gain(k, shape):
        return 1.0 + 0.02 * jax.random.normal(k, shape, f32)

    dt = jnp.exp(jax.random.uniform(ks[5], (DEPTH, N_V_HEADS_A), f32, math.log(1e-3), math.log(1e-1)))
    return {
        "x": nrm(ks[0], (BATCH, SEQ, D_MODEL), 1.0),
        "p": nrm(ks[1], (DEPTH, BATCH, SEQ, PLE_DIM), 1.0),
        "norm_mix": gain(ks[2], (DEPTH, D_MODEL)),
        "w_in": nrm(ks[3], (DEPTH, D_MODEL, N_IN), D_MODEL ** -0.5),
        "conv_qkv": nrm(ks[6], (DEPTH, CONV_A, 2 * QK_WIDTH_A + V_WIDTH_A), CONV_A ** -0.5),
        "a_log": jnp.log(jax.random.uniform(ks[4], (DEPTH, N_V_HEADS_A), f32, 1.0, 16.0)),
        "dt_bias": dt + jnp.log(-jnp.expm1(-dt)),
        "head_norm": gain(ks[7], (DEPTH, HEAD_DIM)),
        "sgu_norm": gain(ks[8], (DEPTH, WIDTH_B)),
        "w_spatial": nrm(ks[9], (DEPTH, N_GROUPS_B, CHUNK_B, CHUNK_B), CHUNK_B ** -0.5),
        "b_spatial": 1.0 + nrm(ks[10], (DEPTH, N_GROUPS_B, CHUNK_B), 0.1),
        "w_branch_a": nrm(ks[11], (DEPTH, V_WIDTH_A, D_MODEL), V_WIDTH_A ** -0.5),
        "w_branch_b": nrm(ks[12], (DEPTH, WIDTH_B, D_MODEL), WIDTH_B ** -0.5),
        "w_out": nrm(ks[13], (DEPTH, D_MODEL, D_MODEL), D_MODEL ** -0.5),
        "norm_ffn": gain(ks[14], (DEPTH, D_MODEL)),
        "w_ffn_up": nrm(ks[15], (DEPTH, D_MODEL, 2 * D_FF), D_MODEL ** -0.5),
        "conv_ffn": nrm(ks[16], (DEPTH, CONV_FFN, 2 * D_FF), CONV_FFN ** -0.5),
        "b_conv_ffn": nrm(ks[17], (DEPTH, 2 * D_FF), 0.02),
        "w_ffn_down": nrm(ks[18], (DEPTH, D_FF, D_MODEL), D_FF ** -0.5),
        "norm_ple": gain(ks[19], (DEPTH, D_MODEL)),
        "w_ple_gate": nrm(ks[20], (DEPTH, D_MODEL, D_MODEL), D_MODEL ** -0.5),
        "w_ple_proj": nrm(ks[21], (DEPTH, PLE_DIM, D_MODEL), PLE_DIM ** -0.5),
        "norm_final": gain(ks[22], (D_MODEL,)),
    }


def reference(x, p, norm_mix, w_in, conv_qkv, a_log, dt_bias, head_norm, sgu_norm, w_spatial,
              b_spatial, w_branch_a, w_branch_b, w_out, norm_ffn, w_ffn_up, conv_ffn, b_conv_ffn,
              w_ffn_down, norm_ple, w_ple_gate, w_ple_proj, norm_final):
    split_idx = np.cumsum(SPLIT_SIZES)[:-1].tolist()
    for i in range(DEPTH):
        h = rms_norm(x, norm_mix[i])
        proj = h @ w_in[i]
        q, k, v, z, b_logit, a_logit, u_b, v_b, gates = jnp.split(proj, split_idx, axis=-1)
        y_a = delta_mixer(q, k, v, z, b_logit, a_logit, conv_qkv[i], a_log[i], dt_bias[i],
                          head_norm[i]) @ w_branch_a[i]
        y_b = spatial_gating_mixer(u_b, v_b, sgu_norm[i], w_spatial[i], b_spatial[i]) @ w_branch_b[i]
        g_a, g_b = jnp.split(gates, N_BRANCH, axis=-1)
        merged = jax.nn.sigmoid(g_a) * y_a + jax.nn.sigmoid(g_b) * y_b
        x = x + merged @ w_out[i]
        h = rms_norm(x, norm_ffn[i])
        x = x + conv_glu_ffn(h, w_ffn_up[i], conv_ffn[i], b_conv_ffn[i], w_ffn_down[i])
        h = rms_norm(x, norm_ple[i])
        x = x + jax.nn.sigmoid(h @ w_ple_gate[i]) * (p[i] @ w_ple_proj[i])
    return rms_norm(x, norm_final)
```

```python
import numpy as np
from contextlib import ExitStack
import concourse.bass as bass
import concourse.mybir as mybir
from concourse.bass_utils import run_bass_kernel_spmd

F32 = mybir.dt.float32
BF16 = mybir.dt.bfloat16
AF = mybir.ActivationFunctionType
ALU = mybir.AluOpType

D = 2048
KC = 16
T = 512
NIN = 12320
DFF = 5632
EPS = 1e-6
ENGS = ("pe", "act", "dve", "pool", "sp")
NDMA_SEM = 24
G_SB = 1024
G_PS = 2048


class V:
    __slots__ = ("ap", "space", "lo", "hi", "shape", "strides", "esz")

    def __init__(self, ap, space, lo, hi, shape, strides, esz):
        self.ap, self.space, self.lo, self.hi = ap, space, lo, hi
        self.shape, self.strides, self.esz = list(shape), list(strides), esz

    def __getitem__(self, idx):
        if not isinstance(idx, tuple):
            idx = (idx,)
        idx = idx + (slice(None),) * (len(self.shape) - len(idx))
        lo = 0
        hi = 0
        nshape, nstr = [], []
        for d, (i, n, st) in enumerate(zip(idx, self.shape, self.strides)):
            if isinstance(i, int):
                a, b, keep = i, i + 1, False
            else:
                a = 0 if i.start is None else i.start
                b = n if i.stop is None else i.stop
                keep = True
            assert 0 <= a < b <= n, (idx, self.shape)
            if d > 0:
                lo += a * st
                hi += (b - 1) * st
            if keep:
                nshape.append(b - a)
                nstr.append(st)
        return V(self.ap[idx], self.space, self.lo + lo * self.esz, self.lo + (hi + 1) * self.esz,
                 nshape, nstr, self.esz)

    def flat(self):
        if len(self.shape) == 2:
            return self
        pat = {3: "p a b -> p (a b)", 4: "p a b c -> p (a b c)"}[len(self.shape)]
        n = 1
        for d in self.shape[1:]:
            n *= d
        return V(self.ap.rearrange(pat), self.space, self.lo, self.hi, [self.shape[0], n], [0, 1], self.esz)

    def toks(self):
        g = G_SB if self.space == "sb" else G_PS
        return [(self.space, i) for i in range(self.lo // g, (self.hi - 1) // g + 1)]


class Arena:
    def __init__(self, nc, stack, name, nbytes, space):
        self.space = space
        if space == "sb":
            self.t = stack.enter_context(nc.sbuf_tensor(name, [128, nbytes // 4], F32))
        else:
            self.t = stack.enter_context(nc.psum_tensor(name, [128, nbytes // 4], F32))
        self.off = 0
        self.cap = nbytes

    def alloc(self, shape, dt, at=None):
        esz = 2 if dt == BF16 else 4
        n = 1
        for d in shape[1:]:
            n *= d
        nb = n * esz
        if at is None:
            off = self.off
            self.off += (nb + 63) // 64 * 64
            assert self.off <= self.cap, ("arena overflow", self.space, self.off, self.cap)
        else:
            off = at
            assert off + nb <= self.cap
        ap = self.t[0:shape[0], off // 4:(off + nb) // 4]
        if dt != F32:
            ap = ap.bitcast(dt)
        fs = shape[1:]
        if len(fs) == 2:
            ap = ap.rearrange("p (a b) -> p a b", a=fs[0])
        elif len(fs) == 3:
            ap = ap.rearrange("p (a b c) -> p a b c", a=fs[0], b=fs[1])
        strides = [0] * len(shape)
        s = 1
        for d in range(len(shape) - 1, 0, -1):
            strides[d] = s
            s *= shape[d]
        return V(ap, self.space, off, off + nb, shape, strides, esz)


def toks_of(items):
    out = []
    for it in items:
        if isinstance(it, V):
            out.extend(it.toks())
        else:
            out.append(it)
    return out


class Sched:
    def __init__(self, nc, stack):
        self.nc = nc
        self.q = {e: [] for e in ENGS}
        self.cnt = {e: 0 for e in ENGS}
        self.sem = {}
        for e in ("pe", "act", "dve", "pool"):
            self.sem[e] = stack.enter_context(nc.semaphore("s_" + e))
        self.dsem = [stack.enter_context(nc.semaphore("d%d" % i)) for i in range(NDMA_SEM)]
        self.dcnt = [0] * NDMA_SEM
        self.dpool = {"pool": list(range(0, 16)), "sp": list(range(16, NDMA_SEM)), "act": list(range(16, NDMA_SEM))}
        self.dnext = {"pool": 0, "sp": 0, "act": 0}
        self.tok = {}
        self.seen = {e: {} for e in ENGS}
        self.nops = 0
        self.plan = False

    def _deps(self, reads, writes):
        deps = {}

        def add(kv):
            if kv is None:
                return
            k, v = kv
            if deps.get(k, 0) < v:
                deps[k] = v

        for t in reads:
            st = self.tok.get(t)
            if st:
                add(st[0])
        for t in writes:
            st = self.tok.get(t)
            if st:
                add(st[0])
                for k, v in st[1].items():
                    add((k, v))
        return deps

    def _mark(self, reads, writes, key, val):
        for t in reads:
            st = self.tok.setdefault(t, [None, {}])
            st[1][key] = val
        for t in writes:
            self.tok[t] = [(key, val), {}]

    def _waits(self, eng, deps):
        out = []
        seen = self.seen[eng]
        for k, v in deps.items():
            if eng == "pe" and k == "pe":
                continue
            if seen.get(k, 0) >= v:
                continue
            seen[k] = v
            out.append((k, v))
        return out

    def op(self, eng, fn, reads=(), writes=()):
        if self.plan:
            return
        reads = toks_of(reads)
        writes = toks_of(writes)
        deps = self._deps(reads, writes)
        waits = self._waits(eng, deps)
        self.cnt[eng] += 1
        val = self.cnt[eng]
        self.q[eng].append((waits, fn, (eng, 1)))
        self._mark(reads, writes, eng, val)
        self.nops += 1

    def dma(self, eng, fn, reads=(), writes=()):
        if self.plan:
            return
        reads = toks_of(reads)
        writes = toks_of(writes)
        pl = self.dpool[eng]
        j = pl[self.dnext[eng] % len(pl)]
        self.dnext[eng] += 1
        key = ("d", j)
        deps = self._deps(reads, writes)
        if self.dcnt[j]:
            if deps.get(key, 0) < self.dcnt[j]:
                deps[key] = self.dcnt[j]
        waits = self._waits(eng, deps)
        self.dcnt[j] += 16
        self.q[eng].append((waits, fn, (key, 16)))
        self._mark(reads, writes, key, self.dcnt[j])
        self.nops += 1

    def finish(self, eng, toks):
        if self.plan:
            return
        deps = self._deps(toks_of(toks), ())
        waits = self._waits(eng, deps)
        self.q[eng].append((waits, None, None))

    def _semh(self, k):
        if isinstance(k, tuple):
            return self.dsem[k[1]]
        return self.sem[k]

    def emit(self):
        nc = self.nc
        emap = {"pe": "tensor", "act": "scalar", "dve": "vector", "pool": "gpsimd", "sp": "sync"}
        with nc.Block() as block:
            for e in ENGS:
                items = self.q[e]

                def body(engobj, items=items):
                    for waits, fn, inc in items:
                        for k, v in waits:
                            engobj.wait_ge(self._semh(k), v)
                        if fn is None:
                            continue
                        ins = fn(engobj)
                        ins.then_inc(self._semh(inc[0]), inc[1])

                getattr(block, emap[e])(body)


C_IDENT = 0
C_TRI = 128
C_U = 192
C_NMS = 256
C_NMI = 320
C_SGM = 384
C_N = 512


def make_consts():
    c = np.zeros((128, C_N), np.float32)
    c[:, C_IDENT:C_IDENT + 128] = np.eye(128, dtype=np.float32)
    i = np.arange(64)
    c[:64, C_TRI:C_TRI + 64] = (i[:, None] <= i[None, :])
    c[:64, C_U:C_U + 64] = (i[:, None] > i[None, :])
    c[:64, C_NMS:C_NMS + 64] = np.where(i[:, None] > i[None, :], 0.0, -30000.0)
    c[:64, C_NMI:C_NMI + 64] = np.where(i[None, :] >= i[:, None], 0.0, -30000.0)
    j = np.arange(128)
    c[:, C_SGM:C_SGM + 128] = (j[:, None] <= j[None, :])
    return c


LP_GMIX = 0
LP_GFFN = 16
LP_GPLE = 32
LP_CONV = 48
LP_DTB = LP_CONV + 128
LP_ALOG = LP_DTB + 16
LP_HG = LP_ALOG + 16
LP_SGUG = LP_HG + 1
LP_CFFN = LP_SGUG + 8
LP_BFFN = LP_CFFN + 264
LP_N = LP_BFFN + 88


class Cfg:
    def __init__(self, nl, ntile, do_a=True, do_b=True, do_ffn=True, do_ple=True, nb=4):
        self.nl, self.ntile = nl, ntile
        self.S = ntile * T
        self.do_a, self.do_b, self.do_ffn, self.do_ple = do_a, do_b, do_ffn, do_ple
        self.nb = nb
        self.final_norm = True
        import os
        self.dstage = int(os.environ.get('DSTAGE', '9'))


def build(cfg):
    nc = bass.Bass("TRN2", target_bir_lowering=False)
    NL, S = cfg.nl, cfg.S
    dr = {}

    def din(name, shape):
        dr[name] = nc.dram_tensor(name, shape, F32, kind="ExternalInput").ap()
        return dr[name]

    x_t = din("x_t", [D, S])
    p_t = din("p_t", [NL, 256, S])
    w_in = din("w_in", [NL, D, NIN])
    w_a = din("w_a", [NL, D, D])
    w_b = din("w_b", [NL, 1024, D])
    w_o = din("w_o", [NL, D, D])
    w_up = din("w_up", [NL, D, 2 * DFF])
    w_dn = din("w_dn", [NL, DFF, D])
    w_g = din("w_g", [NL, D, D])
    w_p = din("w_p", [NL, 256, D])
    lpar = din("lpar", [NL, 128, LP_N])
    wsT = din("wsT", [NL, 128, 8 * 128])
    brep = din("brep", [NL, 128, 8 * 128])
    gfin = din("gfin", [128, 16])
    cst = din("cst", [128, C_N])
    out_t = nc.dram_tensor("out_t", [D, S], F32, kind="ExternalOutput").ap()

    with ExitStack() as st:
        s = Sched(nc, st)
        SB = Arena(nc, st, "sb", 207 * 1024, "sb")
        PS = Arena(nc, st, "ps", 16 * 1024, "ps")

        def bank(b, shape, dt=F32, off=0):
            return PS.alloc(shape, dt, at=b * 2048 + off)

        F0 = bank(0, [128, 512])
        F1 = bank(1, [128, 512])
        F2 = bank(2, [128, 512])
        B3 = bank(3, [128, 512])
        B4 = bank(4, [128, 512])
        B5 = bank(5, [128, 512])
        B6 = bank(6, [128, 512])
        B7 = bank(7, [128, 512])
        B6h = bank(6, [128, 1024], BF16)
        B7h = bank(7, [128, 1024], BF16)

        cs = SB.alloc([128, C_N], F32)
        identb = SB.alloc([128, 128], BF16)
        onesb = SB.alloc([128, 128], BF16)
        onesf = SB.alloc([64, 128], F32)
        epsT = SB.alloc([128, 1], F32)
        gfinT = SB.alloc([128, 16], F32)
        lp = SB.alloc([128, LP_N], F32)
        nA = SB.alloc([64, 16], F32)
        wsm = SB.alloc([128, 8, 128], BF16)
        brp = SB.alloc([128, 8, 128], F32)
        xT = SB.alloc([128, KC, T], F32)
        hT = SB.alloc([128, KC, T], BF16)
        Sst = SB.alloc([128, NL, 16, 128], F32)
        Sbf = SB.alloc([128, 16, 128], BF16)
        halo = SB.alloc([128, NL, 32, 3], F32)
        fhalo = SB.alloc([128, NL, 88, 2], F32)
        oTg = SB.alloc([128, 16, T], BF16)
        wbuf = [SB.alloc([128, 8, 512], BF16) for _ in range(cfg.nb)]
        r1 = SB.off
        qkT = SB.alloc([128, 8, T], BF16)
        vT = SB.alloc([128, 8, T], BF16)
        scrA = SB.off
        SB.off += 16384
        actb = SB.alloc([128, 22, T], BF16, at=r1)
        r2 = SB.off
        usg = SB.alloc([128, 8, T], BF16)
        vg = SB.alloc([128, 8, T], BF16)
        SB.off += 8192
        merged = SB.alloc([128, 16, T], BF16, at=r2 + 8192)
        sqb = [SB.alloc([128, T], BF16) for _ in range(2)]
        cb = [SB.alloc([128, T + 4], F32) for _ in range(2)]
        acc = [SB.alloc([128, T], F32) for _ in range(2)]
        rs = SB.alloc([128, T], F32)
        sg = SB.alloc([128, T], F32)
        pT = SB.alloc([128, 2, T], BF16)
        vtok = SB.alloc([128, 4, 128], BF16)
        c_eb = SB.alloc([64, 8, 16], F32)
        c_beta = SB.alloc([64, 8, 16], F32)
        c_negb = SB.alloc([64, 8, 16], F32)
        c_g = SB.alloc([64, 8, 16], F32)
        c_gam = SB.alloc([64, 8, 16], F32)
        c_e2 = SB.alloc([64, 8, 16], F32)
        c_be1 = SB.alloc([64, 8, 16], F32)
        c_tmp = SB.alloc([64, 8, 16], F32)
        c_dec = SB.alloc([128, 8, 16], F32)
        c_ghi = SB.alloc([64, 8, 16], BF16)
        c_glo = SB.alloc([64, 8, 16], BF16)
        cb16 = SB.alloc([64, 256], BF16)

        class Carve:
            def __init__(self, regions):
                self.regions = [[a, b] for a, b in regions]

            def alloc(self, shape, dt):
                esz = 2 if dt == BF16 else 4
                n = esz
                for d in shape[1:]:
                    n *= d
                n = (n + 63) // 64 * 64
                for r in self.regions:
                    if r[1] - r[0] >= n:
                        v = SB.alloc(shape, dt, at=r[0])
                        r[0] += n
                        return v
                raise AssertionError("carve overflow")

        cv = Carve([(scrA, scrA + 16384), (r2, r2 + 24576)])
        d_gtri = cv.alloc([64, 8, 64], BF16)
        d_gtrl = cv.alloc([64, 8, 64], BF16)
        d_Em = cv.alloc([64, 8, 64], F32)
        d_EmT = cv.alloc([64, 8, 64], F32)
        d_E1 = cv.alloc([128, 8, 64], F32)
        d_t1 = cv.alloc([64, 8, 64], F32)
        d_A = [cv.alloc([64, 8, 64], BF16) for _ in range(2)]
        d_B = [cv.alloc([64, 8, 64], BF16) for _ in range(2)]
        d_P = [cv.alloc([64, 8, 64], BF16) for _ in range(2)]
        d_kbg = cv.alloc([64, 8, 128], BF16)
        d_kdec = cv.alloc([64, 8, 128], BF16)
        d_bv = cv.alloc([64, 8, 128], BF16)
        d_attnT = cv.alloc([64, 8, 64], BF16)
        d_qdT = cv.alloc([128, 8, 64], BF16)
        d_wT = cv.alloc([128, 8, 64], BF16)
        d_u = cv.alloc([64, 8, 128], F32)
        d_vn = cv.alloc([64, 8, 128], BF16)
        d_on = cv.alloc([128, 8, 64], F32)
        d_sqo = cv.alloc([128, 8, 64], BF16)
        d_rr = cv.alloc([128, 8, 64], F32)
        d_kq = cv.alloc([64, 512], F32)
        d_sn = SB.alloc([128, 128], F32)
        cv2 = Carve([(scrA, scrA + 16384)])
        m_sga = cv2.alloc([128, 4, T], BF16)
        m_sgb = cv2.alloc([128, 4, T], BF16)
        m_mt = cv2.alloc([128, 4, T], F32)

        def act_op(out, in_, func, reads=None, bias=None, scale=None):
            kw = {}
            rd = [in_] if reads is None else list(reads)
            if bias is not None:
                kw["bias"] = bias.ap if isinstance(bias, V) else bias
                if isinstance(bias, V):
                    rd.append(bias)
            if scale is not None:
                kw["scale"] = scale.ap if isinstance(scale, V) else scale
                if isinstance(scale, V):
                    rd.append(scale)
            s.op("act", lambda e: e.activation(out=out.ap, in_=in_.ap, func=func, **kw), reads=rd, writes=[out])

        def dve_tt(out, a, b, op, aap=None, bap=None, oap=None, eng="dve"):
            s.op(eng, lambda e: e.tensor_tensor(out=oap if oap is not None else out.ap,
                                                in0=aap if aap is not None else a.ap,
                                                in1=bap if bap is not None else b.ap, op=op),
                 reads=[a, b], writes=[out])

        def dve_stt(out, a, scalar, b, op0, op1, oap=None, aap=None, bap=None):
            rd = [a, b]
            sc = scalar
            if isinstance(scalar, V):
                rd.append(scalar)
                sc = scalar.ap
            s.op("dve", lambda e: e.scalar_tensor_tensor(out=oap if oap is not None else out.ap,
                                                         in0=aap if aap is not None else a.ap, scalar=sc,
                                                         in1=bap if bap is not None else b.ap, op0=op0, op1=op1),
                 reads=rd, writes=[out])

        def dve_ts(out, a, scalar, op):
            rd = [a]
            sc = scalar
            if isinstance(scalar, V):
                rd.append(scalar)
                sc = scalar.ap
            s.op("dve", lambda e: e.tensor_single_scalar(out=out.ap, in_=a.ap, scalar=sc, op=op),
                 reads=rd, writes=[out])

        def dve_copy(out, a):
            s.op("dve", lambda e: e.tensor_copy(out=out.ap, in_=a.ap), reads=[a], writes=[out])

        def dve_recip(out, a):
            s.op("dve", lambda e: e.reciprocal(out=out.ap, in_=a.ap), reads=[a], writes=[out])

        def mm(out, pairs, start=True, stop=True):
            rd = []
            for a, b in pairs:
                rd.append(a)
                rd.append(b)
            n = len(pairs)

            def fn(e):
                ins = None
                for i, (a, b) in enumerate(pairs):
                    ins = e.matmul(out.ap, lhsT=a.ap, rhs=b.ap, start=(start and i == 0),
                                   stop=(stop and i == n - 1))
                return ins
            s.op("pe", fn, reads=rd, writes=[out])

        def tr(out, in_, ident):
            s.op("pe", lambda e: e.transpose(out=out.ap, in_=in_.ap, identity=ident.ap),
                 reads=[in_, ident], writes=[out])

        class WStream:
            def __init__(self):
                self.descs = []
                self.pos = 0
                self.issued = 0
                self.released = 0

            def reset(self):
                self.pos = 0
                self.issued = 0
                self.released = 0

            def _pump(self):
                while self.issued < min(len(self.descs), self.released + cfg.nb):
                    self._issue(self.issued)
                    self.issued += 1

            def release(self, n=1):
                if s.plan:
                    return
                self.released += n
                self._pump()

            def get(self, srcs, kc):
                if not isinstance(srcs, (list, tuple)):
                    srcs = [srcs]
                cols = sum(a.shape[1] for a in srcs)
                i = self.pos
                self.pos += 1
                slot = wbuf[i % cfg.nb]
                if s.plan:
                    self.descs.append((srcs, kc))
                    return slot[:, 0:kc, 0:cols]
                self._pump()
                assert i < self.issued, "too many live weight blocks"
                return slot[:, 0:kc, 0:cols]

            def _issue(self, i):
                srcs, kc = self.descs[i]
                c0 = 0
                for src in srcs:
                    cols = src.shape[1]
                    dst = wbuf[i % cfg.nb][:, 0:kc, c0:c0 + cols]
                    sv = src.rearrange("(kc p) c -> p kc c", p=128)
                    s.dma("pool", lambda e, dst=dst, sv=sv: e.dma_start(out=dst.ap, in_=sv), writes=[dst])
                    c0 += cols

        W = WStream()

        def wget2(mat, c0, cols):
            a = W.get(mat[0:1024, c0:c0 + cols], 8)
            b = W.get(mat[1024:2048, c0:c0 + cols], 8)
            return a, b

        def dense16(out_ps, wa, wb, oc, rhsT):
            pairs = []
            for kc in range(8):
                pairs.append((wa[:, kc, oc * 128:(oc + 1) * 128], rhsT[:, kc, :]))
            for kc in range(8):
                pairs.append((wb[:, kc, oc * 128:(oc + 1) * 128], rhsT[:, 8 + kc, :]))
            mm(out_ps, pairs)

        def rmsnorm_h(gain_col0):
            for kc in range(KC):
                sq = sqb[kc % 2]
                act_op(sq, xT[:, kc, :], AF.Square)
                mm(F2, [(onesb, sq)], start=(kc == 0), stop=(kc == KC - 1))
            act_op(rs, F2, AF.Sqrt, bias=epsT, scale=1.0 / D)
            dve_recip(rs, rs)
            for kc in range(KC):
                dve_stt(hT[:, kc, :], xT[:, kc, :], lp[:, gain_col0 + kc:gain_col0 + kc + 1], rs, ALU.mult, ALU.mult)

        def setup():
            s.dma("sp", lambda e: e.dma_start(out=cs.ap, in_=cst), writes=[cs])
            s.dma("sp", lambda e: e.dma_start(out=gfinT.ap, in_=gfin), writes=[gfinT])
            dve_copy(identb, cs[:, C_IDENT:C_IDENT + 128])
            dve_copy(cb16, cs[0:64, C_TRI:C_TRI + 256])
            s.op("dve", lambda e: e.memset(onesb.ap, 1.0), writes=[onesb])
            s.op("dve", lambda e: e.memset(onesf.ap, 1.0), writes=[onesf])
            s.op("dve", lambda e: e.memset(epsT.ap, EPS), writes=[epsT])
            s.op("dve", lambda e: e.memset(Sst.ap, 0.0), writes=[Sst])
            s.op("dve", lambda e: e.memset(halo.ap, 0.0), writes=[halo])
            s.op("dve", lambda e: e.memset(fhalo.ap, 0.0), writes=[fhalo])

        def load_layer_params(l):
            s.dma("sp", lambda e: e.dma_start(out=lp.ap, in_=lpar[l]), writes=[lp])
            if cfg.do_a:
                act_op(nA, lp[0:64, LP_ALOG:LP_ALOG + 16], AF.Exp)
                dve_ts(nA, nA, -1.0, ALU.mult)
            if cfg.do_b:
                for half in range(2):
                    stg = acc[half]
                    s.dma("sp", lambda e, half=half, stg=stg: e.dma_start(
                        out=stg.ap, in_=wsT[l, :, half * 512:(half + 1) * 512]), writes=[stg])
                    for gg in range(4):
                        g = half * 4 + gg
                        dve_tt(wsm[:, g, :], stg[:, gg * 128:(gg + 1) * 128], cs[:, C_SGM:C_SGM + 128], ALU.mult)
                s.dma("sp", lambda e: e.dma_start(out=brp.ap.rearrange("p g t -> p (g t)"), in_=brep[l]),
                      writes=[brp])

        tri64 = cb16[:, 0:64]
        U64 = cb16[:, 64:128]
        nms = cb16[:, 128:192]
        nmi = cb16[:, 192:256]
        identb64 = identb[0:64, 0:64]

        def chunk_scalars(l, c):
            wa_, wb_ = W_ba
            pairs = []
            for kc in range(8):
                pairs.append((hT[:, kc, c * 64:(c + 1) * 64], wa_[:, kc, :]))
            for kc in range(8):
                pairs.append((hT[:, 8 + kc, c * 64:(c + 1) * 64], wb_[:, kc, :]))
            ps = F2[0:64, 0:32]
            mm(ps, pairs)
            eb, beta, negb = c_eb[:, c, :], c_beta[:, c, :], c_negb[:, c, :]
            g, gam, e2, be1, tmp = c_g[:, c, :], c_gam[:, c, :], c_e2[:, c, :], c_be1[:, c, :], c_tmp[:, c, :]
            act_op(eb, ps[:, 0:16], AF.Exp, scale=-1.0)
            dve_tt(tmp, ps[:, 16:32], lp[0:64, LP_DTB:LP_DTB + 16], ALU.add)
            dve_ts(eb, eb, 1.0, ALU.add)
            dve_recip(beta, eb)
            dve_ts(negb, beta, -1.0, ALU.mult)
            act_op(tmp, tmp, AF.Exp)
            act_op(tmp, tmp, AF.Ln, bias=1.0)
            dve_tt(g, tmp, nA, ALU.mult)
            ghi, glo = c_ghi[:, c, :], c_glo[:, c, :]
            dve_copy(ghi, g)
            dve_tt(tmp, g, ghi, ALU.subtract)
            dve_copy(glo, tmp)
            pg = F2[0:64, 64:80]
            pl = F2[0:64, 96:112]
            pd = F2[0:128, 128:144]
            mm(pg, [(tri64, ghi), (tri64, glo)])
            mm(pl, [(onesb[0:64, 0:64], ghi), (onesb[0:64, 0:64], glo)])
            mm(pd, [(onesb[0:64, :], ghi), (onesb[0:64, :], glo)])
            act_op(be1, pg, AF.Exp)
            act_op(gam, pg, AF.Copy)
            dve_tt(tmp, pl, gam, ALU.subtract)
            act_op(e2, tmp, AF.Exp)
            dve_tt(be1, be1, beta, ALU.mult)
            act_op(c_dec[:, c, :], pd, AF.Exp)

        def bc_heads(v):
            return v.ap.unsqueeze(1).to_broadcast([v.shape[0], 8, v.shape[1]])

        def delta_chunk(l, c, hg):
            h0 = hg * 8
            j0 = hg * 4
            tsl = slice(c * 64, (c + 1) * 64)
            for jj in range(4):
                tr(B7h[0:64, jj * 128:(jj + 1) * 128], qkT[:, 4 + jj, tsl], identb)
            ktok = B7h[0:64, 0:512]
            kin = ktok.ap.rearrange("p (j d) -> p j d", j=4).unsqueeze(2).to_broadcast([64, 4, 2, 128])

            def sc_bc(v):
                return v.ap.rearrange("p (j r) -> p j r", r=2).unsqueeze(3).to_broadcast([64, 4, 2, 128])
            dve_tt(d_kbg, ktok, c_be1[:, c, h0:h0 + 8], ALU.mult, aap=kin, bap=sc_bc(c_be1[:, c, h0:h0 + 8]),
                   oap=d_kbg.ap.rearrange("p (j r) d -> p j r d", r=2))
            dve_tt(d_kdec, ktok, c_e2[:, c, h0:h0 + 8], ALU.mult, aap=kin, bap=sc_bc(c_e2[:, c, h0:h0 + 8]),
                   oap=d_kdec.ap.rearrange("p (j r) d -> p j r d", r=2))
            for hh in range(8):
                tr(B6h[0:64, hh * 128:(hh + 1) * 128], vT[:, hh, tsl], identb)
            vtk = B6h[0:64, 0:1024]
            dve_tt(d_bv, vtk, c_beta[:, c, h0:h0 + 8], ALU.mult,
                   aap=vtk.ap.rearrange("p (h d) -> p h d", h=8),
                   bap=c_beta[:, c, h0:h0 + 8].ap.unsqueeze(2).to_broadcast([64, 8, 128]))
            if cfg.dstage <= 1:
                return
            for gsrc, gdst in ((c_ghi, d_gtri), (c_glo, d_gtrl)):
                gsl = gsrc[:, c, h0:h0 + 8]
                dve_tt(gdst, tri64, gsl, ALU.mult, aap=bc_heads(tri64),
                       bap=gsl.ap.unsqueeze(2).to_broadcast([64, 8, 64]))
            ones64 = onesb[0:64, :]
            for hh in range(8):
                hs = slice(hh * 64, (hh + 1) * 64)
                gh, gl = d_gtri[:, hh, :], d_gtrl[:, hh, :]
                mm(B3[0:64, hs], [(gh, U64), (gl, U64), (identb64, nms)])
                mm(B4[0:64, hs], [(U64, gh), (U64, gl), (identb64, nmi)])
                mm(B5[0:128, hs], [(ones64, gh), (ones64, gl)])
            act_op(d_Em.flat(), B3[0:64, :], AF.Exp)
            act_op(d_EmT.flat(), B4[0:64, :], AF.Exp)
            act_op(d_E1.flat(), B5, AF.Exp)
            if cfg.dstage <= 2:
                return
            for jj in range(4):
                kTc = qkT[:, 4 + jj, tsl]
                qTc = qkT[:, jj, tsl]
                mm(B6[0:64, jj * 64:(jj + 1) * 64], [(kTc, kTc)])
                mm(B6[0:64, 256 + jj * 64:256 + (jj + 1) * 64], [(kTc, qTc)])
            nb_ = c_negb[:, c, h0:h0 + 8]
            dve_tt(d_t1, d_Em, nb_, ALU.mult, bap=nb_.ap.unsqueeze(2).to_broadcast([64, 8, 64]))
            kk = B6[0:64, 0:256]
            qk = B6[0:64, 256:512]

            def pair_bc(v):
                return v.ap.rearrange("p (j s) -> p j s", j=4).unsqueeze(2).to_broadcast([64, 4, 2, 64])

            def as4(v):
                return v.ap.rearrange("p (j r) s -> p j r s", r=2)
            act_op(d_kq, B6[0:64, :], AF.Copy)
            for jj in range(4):
                kkj = d_kq[:, jj * 64:(jj + 1) * 64]
                qkj = d_kq[:, 256 + jj * 64:256 + (jj + 1) * 64]
                hp = slice(2 * jj, 2 * jj + 2)
                dve_tt(d_A[0][:, hp, :], d_t1[:, hp, :], kkj, ALU.mult,
                       bap=kkj.ap.unsqueeze(1).to_broadcast([64, 2, 64]))
                dve_tt(d_attnT[:, hp, :], d_EmT[:, hp, :], qkj, ALU.mult,
                       bap=qkj.ap.unsqueeze(1).to_broadcast([64, 2, 64]))
                qj = qkT[:, jj, tsl]
                dve_tt(d_qdT[:, hp, :], d_E1[:, hp, :], qj, ALU.mult,
                       bap=qj.ap.unsqueeze(1).to_broadcast([128, 2, 64]))
            if cfg.dstage <= 3:
                return
            for hh in range(8):
                tr(B7h[0:64, 512 + hh * 64:512 + (hh + 1) * 64], d_A[0][:, hh, :], identb64)
            ntp = B7h[0:64, 512:1024]
            act_op(d_B[0].flat(), ntp, AF.Copy)
            dve_tt(d_P[0], d_B[0], identb64, ALU.add, bap=bc_heads(identb64))
            for j in range(5):
                a_, b_ = d_A[j % 2], d_B[j % 2]
                an, bn = d_A[(j + 1) % 2], d_B[(j + 1) % 2]
                for hh in range(8):
                    hs = slice(hh * 64, (hh + 1) * 64)
                    mm(B3[0:64, hs], [(b_[:, hh, :], a_[:, hh, :])])
                if j < 4:
                    for hh in range(8):
                        hs = slice(hh * 64, (hh + 1) * 64)
                        mm(B4[0:64, hs], [(a_[:, hh, :], b_[:, hh, :])])
                act_op(an.flat(), B3[0:64, :], AF.Copy)
                if j < 4:
                    dve_copy(bn.flat(), B4[0:64, :])
                pj, pn = d_P[j % 2], d_P[(j + 1) % 2]
                for hh in range(8):
                    hs = slice(hh * 64, (hh + 1) * 64)
                    mm(B5[0:64, hs], [(an[:, hh, :], pj[:, hh, :])])
                dve_tt(pn.flat(), pj.flat(), B5[0:64, :], ALU.add)
            TT = d_P[1]
            if cfg.dstage <= 4:
                return
            for hh in range(8):
                mm(B5[0:128, hh * 64:(hh + 1) * 64], [(d_kbg[:, hh, :], TT[:, hh, :])])
            act_op(d_wT.flat(), B5, AF.Copy)
            for sb_ in range(2):
                for q in range(4):
                    hh = sb_ * 4 + q
                    mm(B6[0:64, q * 128:(q + 1) * 128], [(TT[:, hh, :], d_bv[:, hh, :])])
                act_op(d_u[:, sb_ * 4:(sb_ + 1) * 4, :].flat(), B6[0:64, :], AF.Copy)
            if cfg.dstage <= 5:
                return
            for sb_ in range(2):
                for q in range(4):
                    hh = sb_ * 4 + q
                    mm(B6[0:64, q * 128:(q + 1) * 128], [(d_wT[:, hh, :], Sbf[:, h0 + hh, :])])
                vnb = d_vn[:, sb_ * 4:(sb_ + 1) * 4, :]
                act_op(d_kq, B6[0:64, :], AF.Copy)
                dve_tt(vnb.flat(), d_u[:, sb_ * 4:(sb_ + 1) * 4, :].flat(), d_kq, ALU.subtract)
                if cfg.dstage == 7:
                    continue
                for q in range(4):
                    hh = sb_ * 4 + q
                    mm(B5[0:128, hh * 64:(hh + 1) * 64], [(Sbf[:, h0 + hh, :], d_qdT[:, hh, :])])
                    mm(B3[0:128, hh * 64:(hh + 1) * 64], [(d_vn[:, hh, :], d_attnT[:, hh, :])])
                for q in range(4):
                    hh = sb_ * 4 + q
                    mm(B7[0:128, q * 128:(q + 1) * 128], [(d_kdec[:, hh, :], d_vn[:, hh, :])])
                if cfg.dstage == 8:
                    continue
                for q in range(4):
                    hh = sb_ * 4 + q
                    sv = Sst[:, l, h0 + hh, :]
                    act_op(d_sn, B7[0:128, q * 128:(q + 1) * 128], AF.Copy)
                    dcol = c_dec[:, c, h0 + hh:h0 + hh + 1]
                    dve_tt(sv, sv, dcol, ALU.mult, bap=dcol.ap.to_broadcast([128, 128]))
                    dve_tt(sv, sv, d_sn, ALU.add)
                    act_op(Sbf[:, h0 + hh, :], sv, AF.Copy)
            if cfg.dstage <= 8:
                return
            act_op(d_on.flat(), B5, AF.Copy)
            dve_tt(d_on.flat(), B3, d_on.flat(), ALU.add)
            act_op(d_sqo.flat(), d_on.flat(), AF.Square)
            mm(F2, [(onesb, d_sqo.flat())])
            act_op(d_rr.flat(), F2, AF.Sqrt, bias=epsT, scale=1.0 / 128)
            dve_recip(d_rr.flat(), d_rr.flat())
            ov = oTg[:, h0:h0 + 8, tsl]
            dve_stt(ov, d_on, lp[:, LP_HG:LP_HG + 1], d_rr, ALU.mult, ALU.mult)

        W_ba = [None, None]

        def mixer_a(l):
            Wl = w_in[l]
            W_ba[0] = W.get(Wl[0:1024, 6144:6176], 8)
            W_ba[1] = W.get(Wl[1024:2048, 6144:6176], 8)
            for c in range(8):
                if cfg.dstage >= -1:
                    chunk_scalars(l, c)
            W.release(2)
            if cfg.dstage <= -1:
                return
            for h in range(16):
                act_op(Sbf[:, h, :], Sst[:, l, h, :], AF.Copy)
            for hg in range(2):
                blocks = [(hg * 512, 0), (1024 + hg * 512, 1), (2048 + hg * 1024, 2), (2048 + hg * 1024 + 512, 3)]
                for c0, kind in blocks:
                    wa_, wb_ = wget2(Wl, c0, 512)
                    for oc in range(4):
                        ci = c0 // 128 + oc
                        ps = F0 if oc % 2 == 0 else F1
                        dense16(ps, wa_, wb_, oc, hT)
                        cbv = cb[oc % 2]
                        ac = acc[oc % 2]
                        act_op(cbv[:, 3:3 + T], ps, AF.Copy)
                        hv = halo[:, l, ci, :]
                        act_op(cbv[:, 0:3], hv, AF.Copy)
                        act_op(hv, cbv[:, T:T + 3], AF.Copy)
                        cw0 = LP_CONV + ci * 4
                        dve_ts(ac, cbv[:, 0:T], lp[:, cw0:cw0 + 1], ALU.mult)
                        for j in range(1, 4):
                            dve_stt(ac, cbv[:, j:j + T], lp[:, cw0 + j:cw0 + j + 1], ac, ALU.mult, ALU.add)
                        if kind < 2:
                            act_op(ac, ac, AF.Silu)
                            sq = sqb[oc % 2]
                            act_op(sq, ac, AF.Square)
                            mm(F2, [(onesb, sq)])
                            act_op(rs, F2, AF.Sqrt, bias=epsT)
                            dve_recip(rs, rs)
                            if kind == 0:
                                dve_stt(qkT[:, oc, :], ac, 128.0 ** -0.5, rs, ALU.mult, ALU.mult)
                            else:
                                dve_tt(qkT[:, 4 + oc, :], ac, rs, ALU.mult)
                        else:
                            act_op(vT[:, (kind - 2) * 4 + oc, :], ac, AF.Silu)
                    W.release(2)
                for c in range(8):
                    if cfg.dstage >= 1:
                        delta_chunk(l, c, hg)
            for blk in range(4):
                wa_, wb_ = wget2(Wl, 4096 + blk * 512, 512)
                for oc in range(4):
                    ci = blk * 4 + oc
                    ps = F0 if ci % 2 == 0 else F1
                    dense16(ps, wa_, wb_, oc, hT)
                    ac = acc[ci % 2]
                    act_op(ac, ps, AF.Silu)
                    dve_tt(oTg[:, ci, :], oTg[:, ci, :], ac, ALU.mult)
                W.release(2)

        def mixer_b(l):
            Wl = w_in[l]
            for blk in range(2):
                wa_, wb_ = wget2(Wl, 6176 + blk * 512, 512)
                for oc in range(4):
                    ci = blk * 4 + oc
                    ps = F0 if ci % 2 == 0 else F1
                    dense16(ps, wa_, wb_, oc, hT)
                    act_op(usg[:, ci, :], ps, AF.Gelu_apprx_tanh)
                W.release(2)
            for blk in range(2):
                wa_, wb_ = wget2(Wl, 7200 + blk * 512, 512)
                for oc in range(4):
                    ci = blk * 4 + oc
                    ps = F0 if ci % 2 == 0 else F1
                    dense16(ps, wa_, wb_, oc, hT)
                    ac = acc[ci % 2]
                    act_op(ac, ps, AF.Gelu_apprx_tanh)
                    sq = sqb[ci % 2]
                    act_op(sq, ac, AF.Square)
                    mm(F2, [(onesb, sq)], start=(ci == 0), stop=(ci == 7))
                    dve_copy(vg[:, ci, :], ac)
                W.release(2)
            act_op(rs, F2, AF.Sqrt, bias=epsT, scale=1.0 / 1024)
            dve_recip(rs, rs)
            for ci in range(8):
                dve_stt(vg[:, ci, :], vg[:, ci, :], lp[:, LP_SGUG + ci:LP_SGUG + ci + 1], rs, ALU.mult, ALU.mult)
            for g in range(8):
                for tc in range(4):
                    tr(B7h[0:128, tc * 128:(tc + 1) * 128], vg[:, g, tc * 128:(tc + 1) * 128], identb)
                act_op(vtok.flat(), B7h[0:128, 0:512], AF.Copy)
                for tc in range(4):
                    mm(B5[0:128, tc * 128:(tc + 1) * 128], [(vtok[:, tc, :], wsm[:, g, :])])
                mx = acc[g % 2]
                dve_tt(mx, B5, brp[:, g, :], ALU.add, aap=B5.ap.rearrange("p (c t) -> p c t", c=4),
                       bap=brp[:, g, :].ap.unsqueeze(1).to_broadcast([128, 4, 128]),
                       oap=mx.ap.rearrange("p (c t) -> p c t", c=4))
                dve_tt(usg[:, g, :], usg[:, g, :], mx, ALU.mult)

        def merge_out(l):
            Wl = w_in[l]
            GA = 8224
            GB = 8224 + 2048
            for blk in range(4):
                wa_, wb_ = wget2(Wl, GA + blk * 512, 512)
                for oc in range(4):
                    ps = F0 if oc % 2 == 0 else F1
                    dense16(ps, wa_, wb_, oc, hT)
                    act_op(m_sga[:, oc, :], ps, AF.Sigmoid)
                W.release(2)
                if cfg.do_a:
                    wa_, wb_ = wget2(w_a[l], blk * 512, 512)
                    for oc in range(4):
                        ps = F0 if oc % 2 == 0 else F1
                        dense16(ps, wa_, wb_, oc, oTg)
                        dve_tt(m_mt[:, oc, :], ps, m_sga[:, oc, :], ALU.mult)
                    W.release(2)
                else:
                    s.op("dve", lambda e: e.memset(m_mt.ap, 0.0), writes=[m_mt])
                wa_, wb_ = wget2(Wl, GB + blk * 512, 512)
                for oc in range(4):
                    ps = F0 if oc % 2 == 0 else F1
                    dense16(ps, wa_, wb_, oc, hT)
                    act_op(m_sgb[:, oc, :], ps, AF.Sigmoid)
                W.release(2)
                if cfg.do_b:
                    wb8 = W.get(w_b[l][:, blk * 512:(blk + 1) * 512], 8)
                    for oc in range(4):
                        ps = F0 if oc % 2 == 0 else F1
                        mm(ps, [(wb8[:, kc, oc * 128:(oc + 1) * 128], usg[:, kc, :]) for kc in range(8)])
                        ac = acc[oc % 2]
                        dve_tt(ac, ps, m_sgb[:, oc, :], ALU.mult)
                        dve_tt(merged[:, blk * 4 + oc, :], m_mt[:, oc, :], ac, ALU.add)
                    W.release(1)
                else:
                    for oc in range(4):
                        dve_copy(merged[:, blk * 4 + oc, :], m_mt[:, oc, :])
            for blk in range(4):
                wa_, wb_ = wget2(w_o[l], blk * 512, 512)
                for oc in range(4):
                    j = blk * 4 + oc
                    ps = F0 if oc % 2 == 0 else F1
                    dense16(ps, wa_, wb_, oc, merged)
                    dve_tt(xT[:, j, :], xT[:, j, :], ps, ALU.add)
                W.release(2)

        def ffn(l):
            rmsnorm_h(LP_GFFN)
            Wu = w_up[l]
            Wd = w_dn[l]
            banks = [F0, F1, F2, B3]
            for half in range(2):
                for blk in range(11):
                    c0 = half * 2816 + blk * 256
                    ha = W.get([Wu[0:1024, c0:c0 + 256], Wu[0:1024, DFF + c0:DFF + c0 + 256]], 8)
                    hb = W.get([Wu[1024:2048, c0:c0 + 256], Wu[1024:2048, DFF + c0:DFF + c0 + 256]], 8)
                    for oc in range(2):
                        jl = blk * 2 + oc
                        jj = half * 22 + jl
                        dense16(F0, ha, hb, oc, hT)
                        dense16(F1, ha, hb, 2 + oc, hT)
                        accs = []
                        for which, ps in ((0, F0), (1, F1)):
                            ch = jj + which * 44
                            cbv = cb[which]
                            ac = acc[which]
                            act_op(cbv[:, 2:2 + T], ps, AF.Copy)
                            hv = fhalo[:, l, ch, :]
                            act_op(cbv[:, 0:2], hv, AF.Copy)
                            act_op(hv, cbv[:, T:T + 2], AF.Copy)
                            cw0 = LP_CFFN + ch * 3
                            dve_ts(ac, cbv[:, 0:T], lp[:, cw0:cw0 + 1], ALU.mult)
                            for j in range(1, 3):
                                dve_stt(ac, cbv[:, j:j + T], lp[:, cw0 + j:cw0 + j + 1], ac, ALU.mult, ALU.add)
                            accs.append(ac)
                        act_op(sg, accs[0], AF.Silu, bias=lp[:, LP_BFFN + jj:LP_BFFN + jj + 1])
                        dve_stt(actb[:, jl, :], accs[1], lp[:, LP_BFFN + 44 + jj:LP_BFFN + 44 + jj + 1], sg,
                                ALU.add, ALU.mult)
                    W.release(2)
                r0 = half * 2816
                for blk in range(4):
                    for kb in range(3):
                        nk = 8 if kb < 2 else 6
                        wv = W.get(Wd[r0 + kb * 1024:r0 + kb * 1024 + nk * 128, blk * 512:(blk + 1) * 512], nk)
                        for oc in range(4):
                            mm(banks[oc], [(wv[:, kc, oc * 128:(oc + 1) * 128], actb[:, kb * 8 + kc, :])
                                           for kc in range(nk)], start=(kb == 0), stop=(kb == 2))
                        W.release(1)
                    for oc in range(4):
                        j = blk * 4 + oc
                        dve_tt(xT[:, j, :], xT[:, j, :], banks[oc], ALU.add)

        def ple(l, t0):
            rmsnorm_h(LP_GPLE)
            pv = p_t[l].rearrange("(kc p) s -> p kc s", p=128)[:, :, t0:t0 + T]
            s.dma("pool", lambda e: e.dma_start(out=pT.ap, in_=pv), writes=[pT])
            for blk in range(4):
                wa_, wb_ = wget2(w_g[l], blk * 512, 512)
                wp_ = W.get(w_p[l][:, blk * 512:(blk + 1) * 512], 2)
                for oc in range(4):
                    j = blk * 4 + oc
                    dense16(F0, wa_, wb_, oc, hT)
                    mm(F1, [(wp_[:, kc, oc * 128:(oc + 1) * 128], pT[:, kc, :]) for kc in range(2)])
                    act_op(sg, F0, AF.Sigmoid)
                    ac = acc[oc % 2]
                    dve_tt(ac, F1, sg, ALU.mult)
                    dve_tt(xT[:, j, :], xT[:, j, :], ac, ALU.add)
                W.release(3)

        def body():
            W.reset()
            setup()
            xv = x_t.rearrange("(kc p) s -> p kc s", p=128)
            ov = out_t.rearrange("(kc p) s -> p kc s", p=128)
            for ti in range(cfg.ntile):
                t0 = ti * T
                s.dma("sp", lambda e, t0=t0: e.dma_start(out=xT.ap, in_=xv[:, :, t0:t0 + T]), writes=[xT])
                for l in range(NL):
                    load_layer_params(l)
                    if cfg.do_a or cfg.do_b:
                        rmsnorm_h(LP_GMIX)
                        if cfg.do_a:
                            mixer_a(l)
                        if cfg.do_b:
                            mixer_b(l)
                        merge_out(l)
                    if cfg.do_ffn:
                        ffn(l)
                    if cfg.do_ple:
                        ple(l, t0)
                if cfg.final_norm:
                    for kc in range(KC):
                        sq = sqb[kc % 2]
                        act_op(sq, xT[:, kc, :], AF.Square)
                        mm(F2, [(onesb, sq)], start=(kc == 0), stop=(kc == KC - 1))
                    act_op(rs, F2, AF.Sqrt, bias=epsT, scale=1.0 / D)
                    dve_recip(rs, rs)
                for kc in range(KC):
                    if cfg.final_norm:
                        yo = acc[kc % 2]
                        dve_stt(yo, xT[:, kc, :], gfinT[:, kc:kc + 1], rs, ALU.mult, ALU.mult)
                    else:
                        yo = xT[:, kc, :]
                    s.dma("sp", lambda e, kc=kc, yo=yo, t0=t0: e.dma_start(out=ov[:, kc, t0:t0 + T], in_=yo.ap),
                          reads=[yo], writes=["out"])
            s.finish("sp", ["out"])

        s.plan = True
        body()
        s.plan = False
        body()
        s.emit()
    return nc, s


def prep_layer_params(inp, layers):
    nl = len(layers)
    lp = np.zeros((nl, 128, LP_N), np.float32)
    for i, l in enumerate(layers):
        lp[i, :, LP_GMIX:LP_GMIX + 16] = inp["norm_mix"][l].reshape(16, 128).T
        lp[i, :, LP_GFFN:LP_GFFN + 16] = inp["norm_ffn"][l].reshape(16, 128).T
        lp[i, :, LP_GPLE:LP_GPLE + 16] = inp["norm_ple"][l].reshape(16, 128).T
        cw = inp["conv_qkv"][l].reshape(4, 32, 128)
        lp[i, :, LP_CONV:LP_CONV + 128] = cw.transpose(2, 1, 0).reshape(128, 128)
        lp[i, :, LP_DTB:LP_DTB + 16] = inp["dt_bias"][l][None, :]
        lp[i, :, LP_ALOG:LP_ALOG + 16] = inp["a_log"][l][None, :]
        lp[i, :, LP_HG] = inp["head_norm"][l]
        lp[i, :, LP_SGUG:LP_SGUG + 8] = inp["sgu_norm"][l].reshape(8, 128).T
        cf = inp["conv_ffn"][l].reshape(3, 88, 128)
        lp[i, :, LP_CFFN:LP_CFFN + 264] = cf.transpose(2, 1, 0).reshape(128, 264)
        lp[i, :, LP_BFFN:LP_BFFN + 88] = inp["b_conv_ffn"][l].reshape(88, 128).T
    return lp


def make_in_map(inp, b, layers, ntile):
    S = ntile * T
    ls = list(layers)
    m = {
        "x_t": np.ascontiguousarray(inp["x"][b, :S].T),
        "p_t": np.ascontiguousarray(inp["p"][ls][:, b, :S].transpose(0, 2, 1)),
        "w_in": np.ascontiguousarray(inp["w_in"][ls]),
        "w_a": np.ascontiguousarray(inp["w_branch_a"][ls]),
        "w_b": np.ascontiguousarray(inp["w_branch_b"][ls]),
        "w_o": np.ascontiguousarray(inp["w_out"][ls]),
        "w_up": np.ascontiguousarray(inp["w_ffn_up"][ls]),
        "w_dn": np.ascontiguousarray(inp["w_ffn_down"][ls]),
        "w_g": np.ascontiguousarray(inp["w_ple_gate"][ls]),
        "w_p": np.ascontiguousarray(inp["w_ple_proj"][ls]),
        "lpar": prep_layer_params(inp, ls),
        "wsT": np.ascontiguousarray(inp["w_spatial"][ls].transpose(0, 3, 1, 2)).reshape(len(ls), 128, 1024),
        "brep": np.ascontiguousarray(np.broadcast_to(inp["b_spatial"][ls].reshape(len(ls), 1, 1024),
                                                     (len(ls), 128, 1024))),
        "gfin": np.ascontiguousarray(inp["norm_final"].reshape(16, 128).T),
        "cst": make_consts(),
    }
    return m


_CACHE = {}


def _get_nc(nl, ntile, final_norm):
    key = (nl, ntile, final_norm)
    if key not in _CACHE:
        cfg = Cfg(nl, ntile)
        cfg.final_norm = final_norm
        _CACHE[key] = build(cfg)[0]
    return _CACHE[key]


def kernel(**inputs):
    inp = {k: np.asarray(v) for k, v in inputs.items()}
    B, S, _ = inp["x"].shape
    depth = inp["w_in"].shape[0]
    ntile = S // T
    x_cur = [None] * B
    for l in range(depth):
        nc = _get_nc(1, ntile, l == depth - 1)
        in_maps = []
        for b in range(B):
            m = make_in_map(inp, b, [l], ntile)
            if x_cur[b] is not None:
                m["x_t"] = x_cur[b]
            in_maps.append(m)
        res = run_bass_kernel_spmd(nc, in_maps, core_ids=list(range(B)))
        x_cur = [np.ascontiguousarray(r["out_t"]) for r in res.results]
    out = np.stack([xc.T for xc in x_cur], axis=0)
    return np.ascontiguousarray(out.astype(np.float32))
```

```python
import numpy as np
from contextlib import ExitStack
import concourse.bass as bass
import concourse.mybir as mybir
from concourse.bass_utils import run_bass_kernel_spmd

F32 = mybir.dt.float32
BF16 = mybir.dt.bfloat16
AF = mybir.ActivationFunctionType
ALU = mybir.AluOpType

D = 2048
KC = 16
T = 512
NIN = 12320
DFF = 5632
EPS = 1e-6
ENGS = ("pe", "act", "dve", "pool", "sp")
NDMA_SEM = 24
G_SB = 1024
G_PS = 2048


class V:
    __slots__ = ("ap", "space", "lo", "hi", "shape", "strides", "esz")

    def __init__(self, ap, space, lo, hi, shape, strides, esz):
        self.ap, self.space, self.lo, self.hi = ap, space, lo, hi
        self.shape, self.strides, self.esz = list(shape), list(strides), esz

    def __getitem__(self, idx):
        if not isinstance(idx, tuple):
            idx = (idx,)
        idx = idx + (slice(None),) * (len(self.shape) - len(idx))
        lo = 0
        hi = 0
        nshape, nstr = [], []
        for d, (i, n, st) in enumerate(zip(idx, self.shape, self.strides)):
            if isinstance(i, int):
                a, b, keep = i, i + 1, False
            else:
                a = 0 if i.start is None else i.start
                b = n if i.stop is None else i.stop
                keep = True
            assert 0 <= a < b <= n, (idx, self.shape)
            if d > 0:
                lo += a * st
                hi += (b - 1) * st
            if keep:
                nshape.append(b - a)
                nstr.append(st)
        return V(self.ap[idx], self.space, self.lo + lo * self.esz, self.lo + (hi + 1) * self.esz,
                 nshape, nstr, self.esz)

    def flat(self):
        if len(self.shape) == 2:
            return self
        pat = {3: "p a b -> p (a b)", 4: "p a b c -> p (a b c)"}[len(self.shape)]
        n = 1
        for d in self.shape[1:]:
            n *= d
        return V(self.ap.rearrange(pat), self.space, self.lo, self.hi, [self.shape[0], n], [0, 1], self.esz)

    def toks(self):
        g = G_SB if self.space == "sb" else G_PS
        return [(self.space, i) for i in range(self.lo // g, (self.hi - 1) // g + 1)]


class Arena:
    def __init__(self, nc, stack, name, nbytes, space):
        self.space = space
        if space == "sb":
            self.t = stack.enter_context(nc.sbuf_tensor(name, [128, nbytes // 4], F32))
        else:
            self.t = stack.enter_context(nc.psum_tensor(name, [128, nbytes // 4], F32))
        self.off = 0
        self.cap = nbytes

    def alloc(self, shape, dt, at=None):
        esz = 2 if dt == BF16 else 4
        n = 1
        for d in shape[1:]:
            n *= d
        nb = n * esz
        if at is None:
            off = self.off
            self.off += (nb + 63) // 64 * 64
            assert self.off <= self.cap, ("arena overflow", self.space, self.off, self.cap)
        else:
            off = at
            assert off + nb <= self.cap
        ap = self.t[0:shape[0], off // 4:(off + nb) // 4]
        if dt != F32:
            ap = ap.bitcast(dt)
        fs = shape[1:]
        if len(fs) == 2:
            ap = ap.rearrange("p (a b) -> p a b", a=fs[0])
        elif len(fs) == 3:
            ap = ap.rearrange("p (a b c) -> p a b c", a=fs[0], b=fs[1])
        strides = [0] * len(shape)
        s = 1
        for d in range(len(shape) - 1, 0, -1):
            strides[d] = s
            s *= shape[d]
        return V(ap, self.space, off, off + nb, shape, strides, esz)


def toks_of(items):
    out = []
    for it in items:
        if isinstance(it, V):
            out.extend(it.toks())
        else:
            out.append(it)
    return out


class Sched:
    def __init__(self, nc, stack):
        self.nc = nc
        self.q = {e: [] for e in ENGS}
        self.cnt = {e: 0 for e in ENGS}
        self.sem = {}
        for e in ("pe", "act", "dve", "pool"):
            self.sem[e] = stack.enter_context(nc.semaphore("s_" + e))
        self.dsem = [stack.enter_context(nc.semaphore("d%d" % i)) for i in range(NDMA_SEM)]
        self.dcnt = [0] * NDMA_SEM
        self.dpool = {"pool": list(range(0, 16)), "sp": list(range(16, NDMA_SEM)), "act": list(range(16, NDMA_SEM))}
        self.dnext = {"pool": 0, "sp": 0, "act": 0}
        self.tok = {}
        self.seen = {e: {} for e in ENGS}
        self.nops = 0
        self.plan = False

    def _deps(self, reads, writes):
        deps = {}

        def add(kv):
            if kv is None:
                return
            k, v = kv
            if deps.get(k, 0) < v:
                deps[k] = v

        for t in reads:
            st = self.tok.get(t)
            if st:
                add(st[0])
        for t in writes:
            st = self.tok.get(t)
            if st:
                add(st[0])
                for k, v in st[1].items():
                    add((k, v))
        return deps

    def _mark(self, reads, writes, key, val):
        for t in reads:
            st = self.tok.setdefault(t, [None, {}])
            st[1][key] = val
        for t in writes:
            self.tok[t] = [(key, val), {}]

    def _waits(self, eng, deps):
        out = []
        seen = self.seen[eng]
        for k, v in deps.items():
            if eng == "pe" and k == "pe":
                continue
            if seen.get(k, 0) >= v:
                continue
            seen[k] = v
            out.append((k, v))
        return out

    def op(self, eng, fn, reads=(), writes=()):
        if self.plan:
            return
        reads = toks_of(reads)
        writes = toks_of(writes)
        deps = self._deps(reads, writes)
        waits = self._waits(eng, deps)
        self.cnt[eng] += 1
        val = self.cnt[eng]
        self.q[eng].append((waits, fn, (eng, 1)))
        self._mark(reads, writes, eng, val)
        self.nops += 1

    def dma(self, eng, fn, reads=(), writes=()):
        if self.plan:
            return
        reads = toks_of(reads)
        writes = toks_of(writes)
        pl = self.dpool[eng]
        j = pl[self.dnext[eng] % len(pl)]
        self.dnext[eng] += 1
        key = ("d", j)
        deps = self._deps(reads, writes)
        if self.dcnt[j]:
            if deps.get(key, 0) < self.dcnt[j]:
                deps[key] = self.dcnt[j]
        waits = self._waits(eng, deps)
        self.dcnt[j] += 16
        self.q[eng].append((waits, fn, (key, 16)))
        self._mark(reads, writes, key, self.dcnt[j])
        self.nops += 1

    def finish(self, eng, toks):
        if self.plan:
            return
        deps = self._deps(toks_of(toks), ())
        waits = self._waits(eng, deps)
        self.q[eng].append((waits, None, None))

    def _semh(self, k):
        if isinstance(k, tuple):
            return self.dsem[k[1]]
        return self.sem[k]

    def emit(self):
        nc = self.nc
        emap = {"pe": "tensor", "act": "scalar", "dve": "vector", "pool": "gpsimd", "sp": "sync"}
        with nc.Block() as block:
            for e in ENGS:
                items = self.q[e]

                def body(engobj, items=items):
                    for waits, fn, inc in items:
                        for k, v in waits:
                            engobj.wait_ge(self._semh(k), v)
                        if fn is None:
                            continue
                        ins = fn(engobj)
                        ins.then_inc(self._semh(inc[0]), inc[1])

                getattr(block, emap[e])(body)


C_IDENT = 0
C_TRI = 128
C_U = 192
C_NMS = 256
C_NMI = 320
C_SGM = 384
C_N = 512


def make_consts():
    c = np.zeros((128, C_N), np.float32)
    c[:, C_IDENT:C_IDENT + 128] = np.eye(128, dtype=np.float32)
    i = np.arange(64)
    c[:64, C_TRI:C_TRI + 64] = (i[:, None] <= i[None, :])
    c[:64, C_U:C_U + 64] = (i[:, None] > i[None, :])
    c[:64, C_NMS:C_NMS + 64] = np.where(i[:, None] > i[None, :], 0.0, -30000.0)
    c[:64, C_NMI:C_NMI + 64] = np.where(i[None, :] >= i[:, None], 0.0, -30000.0)
    j = np.arange(128)
    c[:, C_SGM:C_SGM + 128] = (j[:, None] <= j[None, :])
    return c


LP_GMIX = 0
LP_GFFN = 16
LP_GPLE = 32
LP_CONV = 48
LP_DTB = LP_CONV + 128
LP_ALOG = LP_DTB + 16
LP_HG = LP_ALOG + 16
LP_SGUG = LP_HG + 1
LP_CFFN = LP_SGUG + 8
LP_BFFN = LP_CFFN + 264
LP_N = LP_BFFN + 88


class Cfg:
    def __init__(self, nl, ntile, do_a=True, do_b=True, do_ffn=True, do_ple=True, nb=4):
        self.nl, self.ntile = nl, ntile
        self.S = ntile * T
        self.do_a, self.do_b, self.do_ffn, self.do_ple = do_a, do_b, do_ffn, do_ple
        self.nb = nb
        self.final_norm = True
        import os
        self.dstage = int(os.environ.get('DSTAGE', '9'))


def build(cfg):
    nc = bass.Bass("TRN2", target_bir_lowering=False)
    NL, S = cfg.nl, cfg.S
    dr = {}

    def din(name, shape):
        dr[name] = nc.dram_tensor(name, shape, F32, kind="ExternalInput").ap()
        return dr[name]

    x_t = din("x_t", [D, S])
    p_t = din("p_t", [NL, 256, S])
    w_in = din("w_in", [NL, D, NIN])
    w_a = din("w_a", [NL, D, D])
    w_b = din("w_b", [NL, 1024, D])
    w_o = din("w_o", [NL, D, D])
    w_up = din("w_up", [NL, D, 2 * DFF])
    w_dn = din("w_dn", [NL, DFF, D])
    w_g = din("w_g", [NL, D, D])
    w_p = din("w_p", [NL, 256, D])
    lpar = din("lpar", [NL, 128, LP_N])
    wsT = din("wsT", [NL, 128, 8 * 128])
    brep = din("brep", [NL, 128, 8 * 128])
    gfin = din("gfin", [128, 16])
    cst = din("cst", [128, C_N])
    out_t = nc.dram_tensor("out_t", [D, S], F32, kind="ExternalOutput").ap()
    st_d = nc.dram_tensor("st_d", [NL, 128, 2048], F32, kind="Internal").ap()

    with ExitStack() as st:
        s = Sched(nc, st)
        SB = Arena(nc, st, "sb", 207 * 1024, "sb")
        PS = Arena(nc, st, "ps", 16 * 1024, "ps")

        def bank(b, shape, dt=F32, off=0):
            return PS.alloc(shape, dt, at=b * 2048 + off)

        F0 = bank(0, [128, 512])
        F1 = bank(1, [128, 512])
        F2 = bank(2, [128, 512])
        B3 = bank(3, [128, 512])
        B4 = bank(4, [128, 512])
        B5 = bank(5, [128, 512])
        B6 = bank(6, [128, 512])
        B7 = bank(7, [128, 512])
        B6h = bank(6, [128, 1024], BF16)
        B7h = bank(7, [128, 1024], BF16)

        cs = SB.alloc([128, C_N], F32)
        identb = SB.alloc([128, 128], BF16)
        onesb = SB.alloc([128, 128], BF16)
        onesf = SB.alloc([64, 128], F32)
        epsT = SB.alloc([128, 1], F32)
        gfinT = SB.alloc([128, 16], F32)
        lp = SB.alloc([128, LP_N], F32)
        nA = SB.alloc([64, 16], F32)
        xT = SB.alloc([128, KC, T], F32)
        hT = SB.alloc([128, KC, T], BF16)
        Sst = SB.alloc([128, 16, 128], F32)
        Sbf = SB.alloc([128, 16, 128], BF16)
        halo = SB.alloc([128, NL, 32, 3], F32)
        fhalo = SB.alloc([128, NL, 88, 2], F32)
        oTg = SB.alloc([128, 16, T], BF16)
        wbuf = [SB.alloc([128, 8, 512], BF16) for _ in range(cfg.nb)]
        r1 = SB.off
        qkT = SB.alloc([128, 8, T], BF16)
        vT = SB.alloc([128, 8, T], BF16)
        scrA = SB.off
        SB.off += 16384
        actb = SB.alloc([128, 22, T], BF16, at=r1)
        r2 = SB.off
        usg = SB.alloc([128, 8, T], BF16)
        vg = SB.alloc([128, 8, T], BF16)
        SB.off += 8192
        merged = SB.alloc([128, 16, T], BF16, at=r2 + 8192)
        sqb = [SB.alloc([128, T], BF16) for _ in range(2)]
        cb = [SB.alloc([128, T + 4], F32) for _ in range(2)]
        acc = [SB.alloc([128, T], F32) for _ in range(2)]
        rs = SB.alloc([128, T], F32)
        c_eb = SB.alloc([64, 8, 16], F32)
        c_beta = SB.alloc([64, 8, 16], F32)
        c_negb = SB.alloc([64, 8, 16], F32)
        c_g = SB.alloc([64, 8, 16], F32)
        c_gam = SB.alloc([64, 8, 16], F32)
        c_e2 = SB.alloc([64, 8, 16], F32)
        c_be1 = SB.alloc([64, 8, 16], F32)
        c_tmp = SB.alloc([64, 8, 16], F32)
        c_dec = SB.alloc([128, 8, 16], F32)
        c_ghi = SB.alloc([64, 8, 16], BF16)
        c_glo = SB.alloc([64, 8, 16], BF16)
        cb16 = SB.alloc([64, 256], BF16)

        class Carve:
            def __init__(self, regions):
                self.regions = [[a, b] for a, b in regions]

            def alloc(self, shape, dt):
                esz = 2 if dt == BF16 else 4
                n = esz
                for d in shape[1:]:
                    n *= d
                n = (n + 63) // 64 * 64
                for r in self.regions:
                    if r[1] - r[0] >= n:
                        v = SB.alloc(shape, dt, at=r[0])
                        r[0] += n
                        return v
                raise AssertionError("carve overflow")

        cv = Carve([(scrA, scrA + 16384), (r2, r2 + 24576)])
        d_gtri = cv.alloc([64, 8, 64], BF16)
        d_gtrl = cv.alloc([64, 8, 64], BF16)
        d_Em = cv.alloc([64, 8, 64], F32)
        d_EmT = cv.alloc([64, 8, 64], F32)
        d_E1 = cv.alloc([128, 8, 64], F32)
        d_t1 = cv.alloc([64, 8, 64], F32)
        d_A = [cv.alloc([64, 8, 64], BF16) for _ in range(2)]
        d_B = [cv.alloc([64, 8, 64], BF16) for _ in range(2)]
        d_P = [cv.alloc([64, 8, 64], BF16) for _ in range(2)]
        d_kbg = cv.alloc([64, 8, 128], BF16)
        d_kdec = cv.alloc([64, 8, 128], BF16)
        d_bv = cv.alloc([64, 8, 128], BF16)
        d_attnT = cv.alloc([64, 8, 64], BF16)
        d_qdT = cv.alloc([128, 8, 64], BF16)
        d_wT = cv.alloc([128, 8, 64], BF16)
        d_u = cv.alloc([64, 8, 128], F32)
        d_vn = cv.alloc([64, 8, 128], BF16)
        d_on = cv.alloc([128, 8, 64], F32)
        d_sqo = cv.alloc([128, 8, 64], BF16)
        d_rr = cv.alloc([128, 8, 64], F32)
        d_kq = cv.alloc([64, 512], F32)
        d_sn = SB.alloc([128, 128], F32)
        cv3 = Carve([(scrA, scrA + 16384)])
        brp = cv3.alloc([128, 8, 128], F32)
        wsm = cv3.alloc([128, 8, 128], BF16)
        vtok = cv3.alloc([128, 4, 128], BF16)
        cv4 = Carve([(r2, r2 + 24576)])
        sg = cv4.alloc([128, T], F32)
        pT = cv4.alloc([128, 2, T], BF16)
        cv2 = Carve([(scrA, scrA + 16384)])
        m_sga = cv2.alloc([128, 4, T], BF16)
        m_sgb = cv2.alloc([128, 4, T], BF16)
        m_mt = cv2.alloc([128, 4, T], F32)

        print('SBUF arena used', SB.off, 'of', SB.cap, flush=True)
        def act_op(out, in_, func, reads=None, bias=None, scale=None):
            kw = {}
            rd = [in_] if reads is None else list(reads)
            if bias is not None:
                kw["bias"] = bias.ap if isinstance(bias, V) else bias
                if isinstance(bias, V):
                    rd.append(bias)
            if scale is not None:
                kw["scale"] = scale.ap if isinstance(scale, V) else scale
                if isinstance(scale, V):
                    rd.append(scale)
            s.op("act", lambda e: e.activation(out=out.ap, in_=in_.ap, func=func, **kw), reads=rd, writes=[out])

        def dve_tt(out, a, b, op, aap=None, bap=None, oap=None, eng="dve"):
            s.op(eng, lambda e: e.tensor_tensor(out=oap if oap is not None else out.ap,
                                                in0=aap if aap is not None else a.ap,
                                                in1=bap if bap is not None else b.ap, op=op),
                 reads=[a, b], writes=[out])

        def dve_stt(out, a, scalar, b, op0, op1, oap=None, aap=None, bap=None):
            rd = [a, b]
            sc = scalar
            if isinstance(scalar, V):
                rd.append(scalar)
                sc = scalar.ap
            s.op("dve", lambda e: e.scalar_tensor_tensor(out=oap if oap is not None else out.ap,
                                                         in0=aap if aap is not None else a.ap, scalar=sc,
                                                         in1=bap if bap is not None else b.ap, op0=op0, op1=op1),
                 reads=rd, writes=[out])

        def dve_ts(out, a, scalar, op):
            rd = [a]
            sc = scalar
            if isinstance(scalar, V):
                rd.append(scalar)
                sc = scalar.ap
            s.op("dve", lambda e: e.tensor_single_scalar(out=out.ap, in_=a.ap, scalar=sc, op=op),
                 reads=rd, writes=[out])

        def dve_copy(out, a):
            s.op("dve", lambda e: e.tensor_copy(out=out.ap, in_=a.ap), reads=[a], writes=[out])

        def dve_recip(out, a):
            s.op("dve", lambda e: e.reciprocal(out=out.ap, in_=a.ap), reads=[a], writes=[out])

        def mm(out, pairs, start=True, stop=True):
            rd = []
            for a, b in pairs:
                rd.append(a)
                rd.append(b)
            n = len(pairs)

            def fn(e):
                ins = None
                for i, (a, b) in enumerate(pairs):
                    ins = e.matmul(out.ap, lhsT=a.ap, rhs=b.ap, start=(start and i == 0),
                                   stop=(stop and i == n - 1))
                return ins
            s.op("pe", fn, reads=rd, writes=[out])

        def tr(out, in_, ident):
            s.op("pe", lambda e: e.transpose(out=out.ap, in_=in_.ap, identity=ident.ap),
                 reads=[in_, ident], writes=[out])

        class WStream:
            def __init__(self):
                self.descs = []
                self.pos = 0
                self.issued = 0
                self.released = 0

            def reset(self):
                self.pos = 0
                self.issued = 0
                self.released = 0

            def _pump(self):
                while self.issued < min(len(self.descs), self.released + cfg.nb):
                    self._issue(self.issued)
                    self.issued += 1

            def release(self, n=1):
                if s.plan:
                    return
                self.released += n
                self._pump()

            def get(self, srcs, kc):
                if not isinstance(srcs, (list, tuple)):
                    srcs = [srcs]
                cols = sum(a.shape[1] for a in srcs)
                i = self.pos
                self.pos += 1
                slot = wbuf[i % cfg.nb]
                if s.plan:
                    self.descs.append((srcs, kc))
                    return slot[:, 0:kc, 0:cols]
                self._pump()
                assert i < self.issued, "too many live weight blocks"
                return slot[:, 0:kc, 0:cols]

            def _issue(self, i):
                srcs, kc = self.descs[i]
                c0 = 0
                for src in srcs:
                    cols = src.shape[1]
                    dst = wbuf[i % cfg.nb][:, 0:kc, c0:c0 + cols]
                    sv = src.rearrange("(kc p) c -> p kc c", p=128)
                    s.dma("pool", lambda e, dst=dst, sv=sv: e.dma_start(out=dst.ap, in_=sv), writes=[dst])
                    c0 += cols

        W = WStream()

        def wget2(mat, c0, cols):
            a = W.get(mat[0:1024, c0:c0 + cols], 8)
            b = W.get(mat[1024:2048, c0:c0 + cols], 8)
            return a, b

        def dense16(out_ps, wa, wb, oc, rhsT):
            pairs = []
            for kc in range(8):
                pairs.append((wa[:, kc, oc * 128:(oc + 1) * 128], rhsT[:, kc, :]))
            for kc in range(8):
                pairs.append((wb[:, kc, oc * 128:(oc + 1) * 128], rhsT[:, 8 + kc, :]))
            mm(out_ps, pairs)

        def rmsnorm_h(gain_col0):
            for kc in range(KC):
                sq = sqb[kc % 2]
                act_op(sq, xT[:, kc, :], AF.Square)
                mm(F2, [(onesb, sq)], start=(kc == 0), stop=(kc == KC - 1))
            act_op(rs, F2, AF.Sqrt, bias=epsT, scale=1.0 / D)
            dve_recip(rs, rs)
            for kc in range(KC):
                dve_stt(hT[:, kc, :], xT[:, kc, :], lp[:, gain_col0 + kc:gain_col0 + kc + 1], rs, ALU.mult, ALU.mult)

        def setup():
            s.dma("sp", lambda e: e.dma_start(out=cs.ap, in_=cst), writes=[cs])
            s.dma("sp", lambda e: e.dma_start(out=gfinT.ap, in_=gfin), writes=[gfinT])
            dve_copy(identb, cs[:, C_IDENT:C_IDENT + 128])
            dve_copy(cb16, cs[0:64, C_TRI:C_TRI + 256])
            s.op("dve", lambda e: e.memset(onesb.ap, 1.0), writes=[onesb])
            s.op("dve", lambda e: e.memset(onesf.ap, 1.0), writes=[onesf])
            s.op("dve", lambda e: e.memset(epsT.ap, EPS), writes=[epsT])
            s.op("dve", lambda e: e.memset(Sst.ap, 0.0), writes=[Sst])
            for l in range(NL):
                s.dma("sp", lambda e, l=l: e.dma_start(out=st_d[l], in_=Sst.ap.rearrange("p h d -> p (h d)")),
                      reads=[Sst], writes=[("st", l)])
            s.op("dve", lambda e: e.memset(halo.ap, 0.0), writes=[halo])
            s.op("dve", lambda e: e.memset(fhalo.ap, 0.0), writes=[fhalo])

        def load_layer_params(l):
            s.dma("sp", lambda e: e.dma_start(out=lp.ap, in_=lpar[l]), writes=[lp])
            if cfg.do_a:
                act_op(nA, lp[0:64, LP_ALOG:LP_ALOG + 16], AF.Exp)
                dve_ts(nA, nA, -1.0, ALU.mult)

        tri64 = cb16[:, 0:64]
        U64 = cb16[:, 64:128]
        nms = cb16[:, 128:192]
        nmi = cb16[:, 192:256]
        identb64 = identb[0:64, 0:64]

        def chunk_scalars(l, c):
            wa_, wb_ = W_ba
            pairs = []
            for kc in range(8):
                pairs.append((hT[:, kc, c * 64:(c + 1) * 64], wa_[:, kc, :]))
            for kc in range(8):
                pairs.append((hT[:, 8 + kc, c * 64:(c + 1) * 64], wb_[:, kc, :]))
            ps = F2[0:64, 0:32]
            mm(ps, pairs)
            eb, beta, negb = c_eb[:, c, :], c_beta[:, c, :], c_negb[:, c, :]
            g, gam, e2, be1, tmp = c_g[:, c, :], c_gam[:, c, :], c_e2[:, c, :], c_be1[:, c, :], c_tmp[:, c, :]
            act_op(eb, ps[:, 0:16], AF.Exp, scale=-1.0)
            dve_tt(tmp, ps[:, 16:32], lp[0:64, LP_DTB:LP_DTB + 16], ALU.add)
            dve_ts(eb, eb, 1.0, ALU.add)
            dve_recip(beta, eb)
            dve_ts(negb, beta, -1.0, ALU.mult)
            act_op(tmp, tmp, AF.Exp)
            act_op(tmp, tmp, AF.Ln, bias=1.0)
            dve_tt(g, tmp, nA, ALU.mult)
            ghi, glo = c_ghi[:, c, :], c_glo[:, c, :]
            dve_copy(ghi, g)
            dve_tt(tmp, g, ghi, ALU.subtract)
            dve_copy(glo, tmp)
            pg = F2[0:64, 64:80]
            pl = F2[0:64, 96:112]
            pd = F2[0:128, 128:144]
            mm(pg, [(tri64, ghi), (tri64, glo)])
            mm(pl, [(onesb[0:64, 0:64], ghi), (onesb[0:64, 0:64], glo)])
            mm(pd, [(onesb[0:64, :], ghi), (onesb[0:64, :], glo)])
            act_op(be1, pg, AF.Exp)
            act_op(gam, pg, AF.Copy)
            dve_tt(tmp, pl, gam, ALU.subtract)
            act_op(e2, tmp, AF.Exp)
            dve_tt(be1, be1, beta, ALU.mult)
            act_op(c_dec[:, c, :], pd, AF.Exp)

        def bc_heads(v):
            return v.ap.unsqueeze(1).to_broadcast([v.shape[0], 8, v.shape[1]])

        def delta_chunk(l, c, hg):
            h0 = hg * 8
            j0 = hg * 4
            tsl = slice(c * 64, (c + 1) * 64)
            for jj in range(4):
                tr(B7h[0:64, jj * 128:(jj + 1) * 128], qkT[:, 4 + jj, tsl], identb)
            ktok = B7h[0:64, 0:512]
            kin = ktok.ap.rearrange("p (j d) -> p j d", j=4).unsqueeze(2).to_broadcast([64, 4, 2, 128])

            def sc_bc(v):
                return v.ap.rearrange("p (j r) -> p j r", r=2).unsqueeze(3).to_broadcast([64, 4, 2, 128])
            dve_tt(d_kbg, ktok, c_be1[:, c, h0:h0 + 8], ALU.mult, aap=kin, bap=sc_bc(c_be1[:, c, h0:h0 + 8]),
                   oap=d_kbg.ap.rearrange("p (j r) d -> p j r d", r=2))
            dve_tt(d_kdec, ktok, c_e2[:, c, h0:h0 + 8], ALU.mult, aap=kin, bap=sc_bc(c_e2[:, c, h0:h0 + 8]),
                   oap=d_kdec.ap.rearrange("p (j r) d -> p j r d", r=2))
            for hh in range(8):
                tr(B6h[0:64, hh * 128:(hh + 1) * 128], vT[:, hh, tsl], identb)
            vtk = B6h[0:64, 0:1024]
            dve_tt(d_bv, vtk, c_beta[:, c, h0:h0 + 8], ALU.mult,
                   aap=vtk.ap.rearrange("p (h d) -> p h d", h=8),
                   bap=c_beta[:, c, h0:h0 + 8].ap.unsqueeze(2).to_broadcast([64, 8, 128]))
            if cfg.dstage <= 1:
                return
            for gsrc, gdst in ((c_ghi, d_gtri), (c_glo, d_gtrl)):
                gsl = gsrc[:, c, h0:h0 + 8]
                dve_tt(gdst, tri64, gsl, ALU.mult, aap=bc_heads(tri64),
                       bap=gsl.ap.unsqueeze(2).to_broadcast([64, 8, 64]))
            ones64 = onesb[0:64, :]
            for hh in range(8):
                hs = slice(hh * 64, (hh + 1) * 64)
                gh, gl = d_gtri[:, hh, :], d_gtrl[:, hh, :]
                mm(B3[0:64, hs], [(gh, U64), (gl, U64), (identb64, nms)])
                mm(B4[0:64, hs], [(U64, gh), (U64, gl), (identb64, nmi)])
                mm(B5[0:128, hs], [(ones64, gh), (ones64, gl)])
            act_op(d_Em.flat(), B3[0:64, :], AF.Exp)
            act_op(d_EmT.flat(), B4[0:64, :], AF.Exp)
            act_op(d_E1.flat(), B5, AF.Exp)
            if cfg.dstage <= 2:
                return
            for jj in range(4):
                kTc = qkT[:, 4 + jj, tsl]
                qTc = qkT[:, jj, tsl]
                mm(B6[0:64, jj * 64:(jj + 1) * 64], [(kTc, kTc)])
                mm(B6[0:64, 256 + jj * 64:256 + (jj + 1) * 64], [(kTc, qTc)])
            nb_ = c_negb[:, c, h0:h0 + 8]
            dve_tt(d_t1, d_Em, nb_, ALU.mult, bap=nb_.ap.unsqueeze(2).to_broadcast([64, 8, 64]))
            kk = B6[0:64, 0:256]
            qk = B6[0:64, 256:512]

            def pair_bc(v):
                return v.ap.rearrange("p (j s) -> p j s", j=4).unsqueeze(2).to_broadcast([64, 4, 2, 64])

            def as4(v):
                return v.ap.rearrange("p (j r) s -> p j r s", r=2)
            act_op(d_kq, B6[0:64, :], AF.Copy)
            for jj in range(4):
                kkj = d_kq[:, jj * 64:(jj + 1) * 64]
                qkj = d_kq[:, 256 + jj * 64:256 + (jj + 1) * 64]
                hp = slice(2 * jj, 2 * jj + 2)
                dve_tt(d_A[0][:, hp, :], d_t1[:, hp, :], kkj, ALU.mult,
                       bap=kkj.ap.unsqueeze(1).to_broadcast([64, 2, 64]))
                dve_tt(d_attnT[:, hp, :], d_EmT[:, hp, :], qkj, ALU.mult,
                       bap=qkj.ap.unsqueeze(1).to_broadcast([64, 2, 64]))
                qj = qkT[:, jj, tsl]
                dve_tt(d_qdT[:, hp, :], d_E1[:, hp, :], qj, ALU.mult,
                       bap=qj.ap.unsqueeze(1).to_broadcast([128, 2, 64]))
            if cfg.dstage <= 3:
                return
            for hh in range(8):
                tr(B7h[0:64, 512 + hh * 64:512 + (hh + 1) * 64], d_A[0][:, hh, :], identb64)
            ntp = B7h[0:64, 512:1024]
            act_op(d_B[0].flat(), ntp, AF.Copy)
            dve_tt(d_P[0], d_B[0], identb64, ALU.add, bap=bc_heads(identb64))
            for j in range(5):
                a_, b_ = d_A[j % 2], d_B[j % 2]
                an, bn = d_A[(j + 1) % 2], d_B[(j + 1) % 2]
                for hh in range(8):
                    hs = slice(hh * 64, (hh + 1) * 64)
                    mm(B3[0:64, hs], [(b_[:, hh, :], a_[:, hh, :])])
                if j < 4:
                    for hh in range(8):
                        hs = slice(hh * 64, (hh + 1) * 64)
                        mm(B4[0:64, hs], [(a_[:, hh, :], b_[:, hh, :])])
                act_op(an.flat(), B3[0:64, :], AF.Copy)
                if j < 4:
                    dve_copy(bn.flat(), B4[0:64, :])
                pj, pn = d_P[j % 2], d_P[(j + 1) % 2]
                for hh in range(8):
                    hs = slice(hh * 64, (hh + 1) * 64)
                    mm(B5[0:64, hs], [(an[:, hh, :], pj[:, hh, :])])
                dve_tt(pn.flat(), pj.flat(), B5[0:64, :], ALU.add)
            TT = d_P[1]
            if cfg.dstage <= 4:
                return
            for hh in range(8):
                mm(B5[0:128, hh * 64:(hh + 1) * 64], [(d_kbg[:, hh, :], TT[:, hh, :])])
            act_op(d_wT.flat(), B5, AF.Copy)
            for sb_ in range(2):
                for q in range(4):
                    hh = sb_ * 4 + q
                    mm(B6[0:64, q * 128:(q + 1) * 128], [(TT[:, hh, :], d_bv[:, hh, :])])
                act_op(d_u[:, sb_ * 4:(sb_ + 1) * 4, :].flat(), B6[0:64, :], AF.Copy)
            if cfg.dstage <= 5:
                return
            for sb_ in range(2):
                for q in range(4):
                    hh = sb_ * 4 + q
                    mm(B6[0:64, q * 128:(q + 1) * 128], [(d_wT[:, hh, :], Sbf[:, h0 + hh, :])])
                vnb = d_vn[:, sb_ * 4:(sb_ + 1) * 4, :]
                act_op(d_kq, B6[0:64, :], AF.Copy)
                dve_tt(vnb.flat(), d_u[:, sb_ * 4:(sb_ + 1) * 4, :].flat(), d_kq, ALU.subtract)
                if cfg.dstage == 7:
                    continue
                for q in range(4):
                    hh = sb_ * 4 + q
                    mm(B5[0:128, hh * 64:(hh + 1) * 64], [(Sbf[:, h0 + hh, :], d_qdT[:, hh, :])])
                    mm(B3[0:128, hh * 64:(hh + 1) * 64], [(d_vn[:, hh, :], d_attnT[:, hh, :])])
                for q in range(4):
                    hh = sb_ * 4 + q
                    mm(B7[0:128, q * 128:(q + 1) * 128], [(d_kdec[:, hh, :], d_vn[:, hh, :])])
                if cfg.dstage == 8:
                    continue
                for q in range(4):
                    hh = sb_ * 4 + q
                    sv = Sst[:, h0 + hh, :]
                    act_op(d_sn, B7[0:128, q * 128:(q + 1) * 128], AF.Copy)
                    dcol = c_dec[:, c, h0 + hh:h0 + hh + 1]
                    dve_tt(sv, sv, dcol, ALU.mult, bap=dcol.ap.to_broadcast([128, 128]))
                    dve_tt(sv, sv, d_sn, ALU.add)
                    act_op(Sbf[:, h0 + hh, :], sv, AF.Copy)
            if cfg.dstage <= 8:
                return
            act_op(d_on.flat(), B5, AF.Copy)
            dve_tt(d_on.flat(), B3, d_on.flat(), ALU.add)
            act_op(d_sqo.flat(), d_on.flat(), AF.Square)
            mm(F2, [(onesb, d_sqo.flat())])
            act_op(d_rr.flat(), F2, AF.Sqrt, bias=epsT, scale=1.0 / 128)
            dve_recip(d_rr.flat(), d_rr.flat())
            ov = oTg[:, h0:h0 + 8, tsl]
            dve_stt(ov, d_on, lp[:, LP_HG:LP_HG + 1], d_rr, ALU.mult, ALU.mult)

        W_ba = [None, None]

        def mixer_a(l):
            Wl = w_in[l]
            s.dma("sp", lambda e: e.dma_start(out=Sst.ap.rearrange("p h d -> p (h d)"), in_=st_d[l]),
                  reads=[("st", l)], writes=[Sst])
            W_ba[0] = W.get(Wl[0:1024, 6144:6176], 8)
            W_ba[1] = W.get(Wl[1024:2048, 6144:6176], 8)
            for c in range(8):
                if cfg.dstage >= -1:
                    chunk_scalars(l, c)
            W.release(2)
            if cfg.dstage <= -1:
                return
            for h in range(16):
                act_op(Sbf[:, h, :], Sst[:, h, :], AF.Copy)
            for hg in range(2):
                blocks = [(hg * 512, 0), (1024 + hg * 512, 1), (2048 + hg * 1024, 2), (2048 + hg * 1024 + 512, 3)]
                for c0, kind in blocks:
                    wa_, wb_ = wget2(Wl, c0, 512)
                    for oc in range(4):
                        ci = c0 // 128 + oc
                        ps = F0 if oc % 2 == 0 else F1
                        dense16(ps, wa_, wb_, oc, hT)
                        cbv = cb[oc % 2]
                        ac = acc[oc % 2]
                        act_op(cbv[:, 3:3 + T], ps, AF.Copy)
                        hv = halo[:, l, ci, :]
                        act_op(cbv[:, 0:3], hv, AF.Copy)
                        act_op(hv, cbv[:, T:T + 3], AF.Copy)
                        cw0 = LP_CONV + ci * 4
                        dve_ts(ac, cbv[:, 0:T], lp[:, cw0:cw0 + 1], ALU.mult)
                        for j in range(1, 4):
                            dve_stt(ac, cbv[:, j:j + T], lp[:, cw0 + j:cw0 + j + 1], ac, ALU.mult, ALU.add)
                        if kind < 2:
                            act_op(ac, ac, AF.Silu)
                            sq = sqb[oc % 2]
                            act_op(sq, ac, AF.Square)
                            mm(F2, [(onesb, sq)])
                            act_op(rs, F2, AF.Sqrt, bias=epsT)
                            dve_recip(rs, rs)
                            if kind == 0:
                                dve_stt(qkT[:, oc, :], ac, 128.0 ** -0.5, rs, ALU.mult, ALU.mult)
                            else:
                                dve_tt(qkT[:, 4 + oc, :], ac, rs, ALU.mult)
                        else:
                            act_op(vT[:, (kind - 2) * 4 + oc, :], ac, AF.Silu)
                    W.release(2)
                for c in range(8):
                    if cfg.dstage >= 1:
                        delta_chunk(l, c, hg)
            s.dma("sp", lambda e: e.dma_start(out=st_d[l], in_=Sst.ap.rearrange("p h d -> p (h d)")),
                  reads=[Sst], writes=[("st", l)])
            for blk in range(4):
                wa_, wb_ = wget2(Wl, 4096 + blk * 512, 512)
                for oc in range(4):
                    ci = blk * 4 + oc
                    ps = F0 if ci % 2 == 0 else F1
                    dense16(ps, wa_, wb_, oc, hT)
                    ac = acc[ci % 2]
                    act_op(ac, ps, AF.Silu)
                    dve_tt(oTg[:, ci, :], oTg[:, ci, :], ac, ALU.mult)
                W.release(2)

        def mixer_b(l):
            Wl = w_in[l]
            for half in range(2):
                stg = acc[half]
                s.dma("sp", lambda e, half=half, stg=stg: e.dma_start(
                    out=stg.ap, in_=wsT[l, :, half * 512:(half + 1) * 512]), writes=[stg])
                for gg in range(4):
                    g = half * 4 + gg
                    dve_tt(wsm[:, g, :], stg[:, gg * 128:(gg + 1) * 128], cs[:, C_SGM:C_SGM + 128], ALU.mult)
            s.dma("sp", lambda e: e.dma_start(out=brp.ap.rearrange("p g t -> p (g t)"), in_=brep[l]),
                  writes=[brp])
            for blk in range(2):
                wa_, wb_ = wget2(Wl, 6176 + blk * 512, 512)
                for oc in range(4):
                    ci = blk * 4 + oc
                    ps = F0 if ci % 2 == 0 else F1
                    dense16(ps, wa_, wb_, oc, hT)
                    act_op(usg[:, ci, :], ps, AF.Gelu_apprx_tanh)
                W.release(2)
            for blk in range(2):
                wa_, wb_ = wget2(Wl, 7200 + blk * 512, 512)
                for oc in range(4):
                    ci = blk * 4 + oc
                    ps = F0 if ci % 2 == 0 else F1
                    dense16(ps, wa_, wb_, oc, hT)
                    ac = acc[ci % 2]
                    act_op(ac, ps, AF.Gelu_apprx_tanh)
                    sq = sqb[ci % 2]
                    act_op(sq, ac, AF.Square)
                    mm(F2, [(onesb, sq)], start=(ci == 0), stop=(ci == 7))
                    dve_copy(vg[:, ci, :], ac)
                W.release(2)
            act_op(rs, F2, AF.Sqrt, bias=epsT, scale=1.0 / 1024)
            dve_recip(rs, rs)
            for ci in range(8):
                dve_stt(vg[:, ci, :], vg[:, ci, :], lp[:, LP_SGUG + ci:LP_SGUG + ci + 1], rs, ALU.mult, ALU.mult)
            for g in range(8):
                for tc in range(4):
                    tr(B7h[0:128, tc * 128:(tc + 1) * 128], vg[:, g, tc * 128:(tc + 1) * 128], identb)
                act_op(vtok.flat(), B7h[0:128, 0:512], AF.Copy)
                for tc in range(4):
                    mm(B5[0:128, tc * 128:(tc + 1) * 128], [(vtok[:, tc, :], wsm[:, g, :])])
                mx = acc[g % 2]
                dve_tt(mx, B5, brp[:, g, :], ALU.add, aap=B5.ap.rearrange("p (c t) -> p c t", c=4),
                       bap=brp[:, g, :].ap.unsqueeze(1).to_broadcast([128, 4, 128]),
                       oap=mx.ap.rearrange("p (c t) -> p c t", c=4))
                dve_tt(usg[:, g, :], usg[:, g, :], mx, ALU.mult)

        def merge_out(l):
            Wl = w_in[l]
            GA = 8224
            GB = 8224 + 2048
            for blk in range(4):
                wa_, wb_ = wget2(Wl, GA + blk * 512, 512)
                for oc in range(4):
                    ps = F0 if oc % 2 == 0 else F1
                    dense16(ps, wa_, wb_, oc, hT)
                    act_op(m_sga[:, oc, :], ps, AF.Sigmoid)
                W.release(2)
                if cfg.do_a:
                    wa_, wb_ = wget2(w_a[l], blk * 512, 512)
                    for oc in range(4):
                        ps = F0 if oc % 2 == 0 else F1
                        dense16(ps, wa_, wb_, oc, oTg)
                        dve_tt(m_mt[:, oc, :], ps, m_sga[:, oc, :], ALU.mult)
                    W.release(2)
                else:
                    s.op("dve", lambda e: e.memset(m_mt.ap, 0.0), writes=[m_mt])
                wa_, wb_ = wget2(Wl, GB + blk * 512, 512)
                for oc in range(4):
                    ps = F0 if oc % 2 == 0 else F1
                    dense16(ps, wa_, wb_, oc, hT)
                    act_op(m_sgb[:, oc, :], ps, AF.Sigmoid)
                W.release(2)
                if cfg.do_b:
                    wb8 = W.get(w_b[l][:, blk * 512:(blk + 1) * 512], 8)
                    for oc in range(4):
                        ps = F0 if oc % 2 == 0 else F1
                        mm(ps, [(wb8[:, kc, oc * 128:(oc + 1) * 128], usg[:, kc, :]) for kc in range(8)])
                        ac = acc[oc % 2]
                        dve_tt(ac, ps, m_sgb[:, oc, :], ALU.mult)
                        dve_tt(merged[:, blk * 4 + oc, :], m_mt[:, oc, :], ac, ALU.add)
                    W.release(1)
                else:
                    for oc in range(4):
                        dve_copy(merged[:, blk * 4 + oc, :], m_mt[:, oc, :])
            for blk in range(4):
                wa_, wb_ = wget2(w_o[l], blk * 512, 512)
                for oc in range(4):
                    j = blk * 4 + oc
                    ps = F0 if oc % 2 == 0 else F1
                    dense16(ps, wa_, wb_, oc, merged)
                    dve_tt(xT[:, j, :], xT[:, j, :], ps, ALU.add)
                W.release(2)

        def ffn(l):
            rmsnorm_h(LP_GFFN)
            Wu = w_up[l]
            Wd = w_dn[l]
            banks = [F0, F1, F2, B3]
            for half in range(2):
                for blk in range(11):
                    c0 = half * 2816 + blk * 256
                    ha = W.get([Wu[0:1024, c0:c0 + 256], Wu[0:1024, DFF + c0:DFF + c0 + 256]], 8)
                    hb = W.get([Wu[1024:2048, c0:c0 + 256], Wu[1024:2048, DFF + c0:DFF + c0 + 256]], 8)
                    for oc in range(2):
                        jl = blk * 2 + oc
                        jj = half * 22 + jl
                        dense16(F0, ha, hb, oc, hT)
                        dense16(F1, ha, hb, 2 + oc, hT)
                        accs = []
                        for which, ps in ((0, F0), (1, F1)):
                            ch = jj + which * 44
                            cbv = cb[which]
                            ac = acc[which]
                            act_op(cbv[:, 2:2 + T], ps, AF.Copy)
                            hv = fhalo[:, l, ch, :]
                            act_op(cbv[:, 0:2], hv, AF.Copy)
                            act_op(hv, cbv[:, T:T + 2], AF.Copy)
                            cw0 = LP_CFFN + ch * 3
                            dve_ts(ac, cbv[:, 0:T], lp[:, cw0:cw0 + 1], ALU.mult)
                            for j in range(1, 3):
                                dve_stt(ac, cbv[:, j:j + T], lp[:, cw0 + j:cw0 + j + 1], ac, ALU.mult, ALU.add)
                            accs.append(ac)
                        act_op(sg, accs[0], AF.Silu, bias=lp[:, LP_BFFN + jj:LP_BFFN + jj + 1])
                        dve_stt(actb[:, jl, :], accs[1], lp[:, LP_BFFN + 44 + jj:LP_BFFN + 44 + jj + 1], sg,
                                ALU.add, ALU.mult)
                    W.release(2)
                r0 = half * 2816
                for blk in range(4):
                    for kb in range(3):
                        nk = 8 if kb < 2 else 6
                        wv = W.get(Wd[r0 + kb * 1024:r0 + kb * 1024 + nk * 128, blk * 512:(blk + 1) * 512], nk)
                        for oc in range(4):
                            mm(banks[oc], [(wv[:, kc, oc * 128:(oc + 1) * 128], actb[:, kb * 8 + kc, :])
                                           for kc in range(nk)], start=(kb == 0), stop=(kb == 2))
                        W.release(1)
                    for oc in range(4):
                        j = blk * 4 + oc
                        dve_tt(xT[:, j, :], xT[:, j, :], banks[oc], ALU.add)

        def ple(l, t0):
            rmsnorm_h(LP_GPLE)
            pv = p_t[l].rearrange("(kc p) s -> p kc s", p=128)[:, :, t0:t0 + T]
            s.dma("pool", lambda e: e.dma_start(out=pT.ap, in_=pv), writes=[pT])
            for blk in range(4):
                wa_, wb_ = wget2(w_g[l], blk * 512, 512)
                wp_ = W.get(w_p[l][:, blk * 512:(blk + 1) * 512], 2)
                for oc in range(4):
                    j = blk * 4 + oc
                    dense16(F0, wa_, wb_, oc, hT)
                    mm(F1, [(wp_[:, kc, oc * 128:(oc + 1) * 128], pT[:, kc, :]) for kc in range(2)])
                    act_op(sg, F0, AF.Sigmoid)
                    ac = acc[oc % 2]
                    dve_tt(ac, F1, sg, ALU.mult)
                    dve_tt(xT[:, j, :], xT[:, j, :], ac, ALU.add)
                W.release(3)

        def body():
            W.reset()
            setup()
            xv = x_t.rearrange("(kc p) s -> p kc s", p=128)
            ov = out_t.rearrange("(kc p) s -> p kc s", p=128)
            for ti in range(cfg.ntile):
                t0 = ti * T
                s.dma("sp", lambda e, t0=t0: e.dma_start(out=xT.ap, in_=xv[:, :, t0:t0 + T]), writes=[xT])
                for l in range(NL):
                    load_layer_params(l)
                    if cfg.do_a or cfg.do_b:
                        rmsnorm_h(LP_GMIX)
                        if cfg.do_a:
                            mixer_a(l)
                        if cfg.do_b:
                            mixer_b(l)
                        merge_out(l)
                    if cfg.do_ffn:
                        ffn(l)
                    if cfg.do_ple:
                        ple(l, t0)
                if cfg.final_norm:
                    for kc in range(KC):
                        sq = sqb[kc % 2]
                        act_op(sq, xT[:, kc, :], AF.Square)
                        mm(F2, [(onesb, sq)], start=(kc == 0), stop=(kc == KC - 1))
                    act_op(rs, F2, AF.Sqrt, bias=epsT, scale=1.0 / D)
                    dve_recip(rs, rs)
                for kc in range(KC):
                    if cfg.final_norm:
                        yo = acc[kc % 2]
                        dve_stt(yo, xT[:, kc, :], gfinT[:, kc:kc + 1], rs, ALU.mult, ALU.mult)
                    else:
                        yo = xT[:, kc, :]
                    s.dma("sp", lambda e, kc=kc, yo=yo, t0=t0: e.dma_start(out=ov[:, kc, t0:t0 + T], in_=yo.ap),
                          reads=[yo], writes=["out"])
            s.finish("sp", ["out"])

        s.plan = True
        body()
        s.plan = False
        body()
        s.emit()
    return nc, s


def prep_layer_params(inp, layers):
    nl = len(layers)
    lp = np.zeros((nl, 128, LP_N), np.float32)
    for i, l in enumerate(layers):
        lp[i, :, LP_GMIX:LP_GMIX + 16] = inp["norm_mix"][l].reshape(16, 128).T
        lp[i, :, LP_GFFN:LP_GFFN + 16] = inp["norm_ffn"][l].reshape(16, 128).T
        lp[i, :, LP_GPLE:LP_GPLE + 16] = inp["norm_ple"][l].reshape(16, 128).T
        cw = inp["conv_qkv"][l].reshape(4, 32, 128)
        lp[i, :, LP_CONV:LP_CONV + 128] = cw.transpose(2, 1, 0).reshape(128, 128)
        lp[i, :, LP_DTB:LP_DTB + 16] = inp["dt_bias"][l][None, :]
        lp[i, :, LP_ALOG:LP_ALOG + 16] = inp["a_log"][l][None, :]
        lp[i, :, LP_HG] = inp["head_norm"][l]
        lp[i, :, LP_SGUG:LP_SGUG + 8] = inp["sgu_norm"][l].reshape(8, 128).T
        cf = inp["conv_ffn"][l].reshape(3, 88, 128)
        lp[i, :, LP_CFFN:LP_CFFN + 264] = cf.transpose(2, 1, 0).reshape(128, 264)
        lp[i, :, LP_BFFN:LP_BFFN + 88] = inp["b_conv_ffn"][l].reshape(88, 128).T
    return lp


def make_in_map(inp, b, layers, ntile):
    S = ntile * T
    ls = list(layers)
    m = {
        "x_t": np.ascontiguousarray(inp["x"][b, :S].T),
        "p_t": np.ascontiguousarray(inp["p"][ls][:, b, :S].transpose(0, 2, 1)),
        "w_in": np.ascontiguousarray(inp["w_in"][ls]),
        "w_a": np.ascontiguousarray(inp["w_branch_a"][ls]),
        "w_b": np.ascontiguousarray(inp["w_branch_b"][ls]),
        "w_o": np.ascontiguousarray(inp["w_out"][ls]),
        "w_up": np.ascontiguousarray(inp["w_ffn_up"][ls]),
        "w_dn": np.ascontiguousarray(inp["w_ffn_down"][ls]),
        "w_g": np.ascontiguousarray(inp["w_ple_gate"][ls]),
        "w_p": np.ascontiguousarray(inp["w_ple_proj"][ls]),
        "lpar": prep_layer_params(inp, ls),
        "wsT": np.ascontiguousarray(inp["w_spatial"][ls].transpose(0, 3, 1, 2)).reshape(len(ls), 128, 1024),
        "brep": np.ascontiguousarray(np.broadcast_to(inp["b_spatial"][ls].reshape(len(ls), 1, 1024),
                                                     (len(ls), 128, 1024))),
        "gfin": np.ascontiguousarray(inp["norm_final"].reshape(16, 128).T),
        "cst": make_consts(),
    }
    return m


_CACHE = {}


def _get_nc(nl, ntile, final_norm):
    key = (nl, ntile, final_norm)
    if key not in _CACHE:
        cfg = Cfg(nl, ntile)
        cfg.final_norm = final_norm
        _CACHE[key] = build(cfg)[0]
    return _CACHE[key]


def kernel(**inputs):
    inp = {k: np.asarray(v) for k, v in inputs.items()}
    B, S, _ = inp["x"].shape
    depth = inp["w_in"].shape[0]
    ntile = S // T
    nc = _get_nc(depth, ntile, True)
    in_maps = [make_in_map(inp, b, range(depth), ntile) for b in range(B)]
    res = run_bass_kernel_spmd(nc, in_maps, core_ids=list(range(B)))
    out = np.stack([r["out_t"].T for r in res.results], axis=0)
    return np.ascontiguousarray(out.astype(np.float32))
```

```python
import numpy as np
from contextlib import ExitStack
import concourse.bass as bass
import concourse.mybir as mybir
from concourse.bass_utils import run_bass_kernel_spmd

F32 = mybir.dt.float32
BF16 = mybir.dt.bfloat16
AF = mybir.ActivationFunctionType
ALU = mybir.AluOpType

D = 2048
KC = 16
T = 512
NIN = 12320
DFF = 5632
EPS = 1e-6
ENGS = ("pe", "act", "dve", "pool", "sp")
NDMA_SEM = 24
G_SB = 1024
G_PS = 2048


class V:
    __slots__ = ("ap", "space", "lo", "hi", "shape", "strides", "esz")

    def __init__(self, ap, space, lo, hi, shape, strides, esz):
        self.ap, self.space, self.lo, self.hi = ap, space, lo, hi
        self.shape, self.strides, self.esz = list(shape), list(strides), esz

    def __getitem__(self, idx):
        if not isinstance(idx, tuple):
            idx = (idx,)
        idx = idx + (slice(None),) * (len(self.shape) - len(idx))
        lo = 0
        hi = 0
        nshape, nstr = [], []
        for d, (i, n, st) in enumerate(zip(idx, self.shape, self.strides)):
            if isinstance(i, int):
                a, b, keep = i, i + 1, False
            else:
                a = 0 if i.start is None else i.start
                b = n if i.stop is None else i.stop
                keep = True
            assert 0 <= a < b <= n, (idx, self.shape)
            if d > 0:
                lo += a * st
                hi += (b - 1) * st
            if keep:
                nshape.append(b - a)
                nstr.append(st)
        return V(self.ap[idx], self.space, self.lo + lo * self.esz, self.lo + (hi + 1) * self.esz,
                 nshape, nstr, self.esz)

    def flat(self):
        if len(self.shape) == 2:
            return self
        pat = {3: "p a b -> p (a b)", 4: "p a b c -> p (a b c)"}[len(self.shape)]
        n = 1
        for d in self.shape[1:]:
            n *= d
        return V(self.ap.rearrange(pat), self.space, self.lo, self.hi, [self.shape[0], n], [0, 1], self.esz)

    def toks(self):
        g = G_SB if self.space == "sb" else G_PS
        return [(self.space, i) for i in range(self.lo // g, (self.hi - 1) // g + 1)]


class Arena:
    def __init__(self, nc, stack, name, nbytes, space):
        self.space = space
        if space == "sb":
            self.t = stack.enter_context(nc.sbuf_tensor(name, [128, nbytes // 4], F32))
        else:
            self.t = stack.enter_context(nc.psum_tensor(name, [128, nbytes // 4], F32))
        self.off = 0
        self.cap = nbytes

    def alloc(self, shape, dt, at=None):
        esz = 2 if dt == BF16 else 4
        n = 1
        for d in shape[1:]:
            n *= d
        nb = n * esz
        if at is None:
            off = self.off
            self.off += (nb + 63) // 64 * 64
            assert self.off <= self.cap, ("arena overflow", self.space, self.off, self.cap)
        else:
            off = at
            assert off + nb <= self.cap
        ap = self.t[0:shape[0], off // 4:(off + nb) // 4]
        if dt != F32:
            ap = ap.bitcast(dt)
        fs = shape[1:]
        if len(fs) == 2:
            ap = ap.rearrange("p (a b) -> p a b", a=fs[0])
        elif len(fs) == 3:
            ap = ap.rearrange("p (a b c) -> p a b c", a=fs[0], b=fs[1])
        strides = [0] * len(shape)
        s = 1
        for d in range(len(shape) - 1, 0, -1):
            strides[d] = s
            s *= shape[d]
        return V(ap, self.space, off, off + nb, shape, strides, esz)


def toks_of(items):
    out = []
    for it in items:
        if isinstance(it, V):
            out.extend(it.toks())
        else:
            out.append(it)
    return out


class Sched:
    def __init__(self, nc, stack):
        self.nc = nc
        self.q = {e: [] for e in ENGS}
        self.cnt = {e: 0 for e in ENGS}
        self.sem = {}
        for e in ("pe", "act", "dve", "pool"):
            self.sem[e] = stack.enter_context(nc.semaphore("s_" + e))
        self.dsem = [stack.enter_context(nc.semaphore("d%d" % i)) for i in range(NDMA_SEM)]
        self.dcnt = [0] * NDMA_SEM
        self.dpool = {"pool": list(range(0, 16)), "sp": list(range(16, NDMA_SEM)), "act": list(range(16, NDMA_SEM))}
        self.dnext = {"pool": 0, "sp": 0, "act": 0}
        self.tok = {}
        self.seen = {e: {} for e in ENGS}
        self.nops = 0
        self.plan = False

    def _deps(self, reads, writes):
        deps = {}

        def add(kv):
            if kv is None:
                return
            k, v = kv
            if deps.get(k, 0) < v:
                deps[k] = v

        for t in reads:
            st = self.tok.get(t)
            if st:
                add(st[0])
        for t in writes:
            st = self.tok.get(t)
            if st:
                add(st[0])
                for k, v in st[1].items():
                    add((k, v))
        return deps

    def _mark(self, reads, writes, key, val):
        for t in reads:
            st = self.tok.setdefault(t, [None, {}])
            st[1][key] = val
        for t in writes:
            self.tok[t] = [(key, val), {}]

    def _waits(self, eng, deps):
        out = []
        seen = self.seen[eng]
        for k, v in deps.items():
            if eng == "pe" and k == "pe":
                continue
            if seen.get(k, 0) >= v:
                continue
            seen[k] = v
            out.append((k, v))
        return out

    def op(self, eng, fn, reads=(), writes=()):
        if self.plan:
            return
        reads = toks_of(reads)
        writes = toks_of(writes)
        deps = self._deps(reads, writes)
        waits = self._waits(eng, deps)
        self.cnt[eng] += 1
        val = self.cnt[eng]
        self.q[eng].append((waits, fn, (eng, 1)))
        self._mark(reads, writes, eng, val)
        self.nops += 1

    def dma(self, eng, fn, reads=(), writes=()):
        if self.plan:
            return
        reads = toks_of(reads)
        writes = toks_of(writes)
        pl = self.dpool[eng]
        j = pl[self.dnext[eng] % len(pl)]
        self.dnext[eng] += 1
        key = ("d", j)
        deps = self._deps(reads, writes)
        if self.dcnt[j]:
            if deps.get(key, 0) < self.dcnt[j]:
                deps[key] = self.dcnt[j]
        waits = self._waits(eng, deps)
        self.dcnt[j] += 16
        self.q[eng].append((waits, fn, (key, 16)))
        self._mark(reads, writes, key, self.dcnt[j])
        self.nops += 1

    def finish(self, eng, toks):
        if self.plan:
            return
        deps = self._deps(toks_of(toks), ())
        waits = self._waits(eng, deps)
        self.q[eng].append((waits, None, None))

    def _semh(self, k):
        if isinstance(k, tuple):
            return self.dsem[k[1]]
        return self.sem[k]

    def emit(self):
        nc = self.nc
        emap = {"pe": "tensor", "act": "scalar", "dve": "vector", "pool": "gpsimd", "sp": "sync"}
        with nc.Block() as block:
            for e in ENGS:
                items = self.q[e]

                def body(engobj, items=items):
                    for waits, fn, inc in items:
                        for k, v in waits:
                            engobj.wait_ge(self._semh(k), v)
                        if fn is None:
                            continue
                        ins = fn(engobj)
                        ins.then_inc(self._semh(inc[0]), inc[1])

                getattr(block, emap[e])(body)


C_IDENT = 0
C_TRI = 128
C_U = 192
C_NMS = 256
C_NMI = 320
C_SGM = 384
C_N = 512


def make_consts():
    c = np.zeros((128, C_N), np.float32)
    c[:, C_IDENT:C_IDENT + 128] = np.eye(128, dtype=np.float32)
    i = np.arange(64)
    c[:64, C_TRI:C_TRI + 64] = (i[:, None] <= i[None, :])
    c[:64, C_U:C_U + 64] = (i[:, None] > i[None, :])
    c[:64, C_NMS:C_NMS + 64] = np.where(i[:, None] > i[None, :], 0.0, -30000.0)
    c[:64, C_NMI:C_NMI + 64] = np.where(i[None, :] >= i[:, None], 0.0, -30000.0)
    j = np.arange(128)
    c[:, C_SGM:C_SGM + 128] = (j[:, None] <= j[None, :])
    return c


LP_GMIX = 0
LP_GFFN = 16
LP_GPLE = 32
LP_CONV = 48
LP_DTB = LP_CONV + 128
LP_ALOG = LP_DTB + 16
LP_HG = LP_ALOG + 16
LP_SGUG = LP_HG + 1
LP_CFFN = LP_SGUG + 8
LP_BFFN = LP_CFFN + 264
LP_N = LP_BFFN + 88


class Cfg:
    def __init__(self, nl, ntile, do_a=True, do_b=True, do_ffn=True, do_ple=True, nb=4):
        self.nl, self.ntile = nl, ntile
        self.S = ntile * T
        self.do_a, self.do_b, self.do_ffn, self.do_ple = do_a, do_b, do_ffn, do_ple
        self.nb = nb
        self.final_norm = True
        import os
        self.dstage = int(os.environ.get('DSTAGE', '9'))


def build(cfg):
    nc = bass.Bass("TRN2", target_bir_lowering=False)
    NL, S = cfg.nl, cfg.S
    dr = {}

    def din(name, shape):
        dr[name] = nc.dram_tensor(name, shape, F32, kind="ExternalInput").ap()
        return dr[name]

    x_t = din("x_t", [D, S])
    p_t = din("p_t", [NL, 256, S])
    w_in = din("w_in", [NL, D, NIN])
    w_a = din("w_a", [NL, D, D])
    w_b = din("w_b", [NL, 1024, D])
    w_o = din("w_o", [NL, D, D])
    w_up = din("w_up", [NL, D, 2 * DFF])
    w_dn = din("w_dn", [NL, DFF, D])
    w_g = din("w_g", [NL, D, D])
    w_p = din("w_p", [NL, 256, D])
    lpar = din("lpar", [NL, 128, LP_N])
    wsT = din("wsT", [NL, 128, 8 * 128])
    brep = din("brep", [NL, 128, 8 * 128])
    gfin = din("gfin", [128, 16])
    cst = din("cst", [128, C_N])
    out_t = nc.dram_tensor("out_t", [D, S], F32, kind="ExternalOutput").ap()
    st_d = nc.dram_tensor("st_d", [NL, 128, 2048], F32, kind="Internal").ap()

    with ExitStack() as st:
        s = Sched(nc, st)
        SB = Arena(nc, st, "sb", 207 * 1024, "sb")
        PS = Arena(nc, st, "ps", 16 * 1024, "ps")

        def bank(b, shape, dt=F32, off=0):
            return PS.alloc(shape, dt, at=b * 2048 + off)

        F0 = bank(0, [128, 512])
        F1 = bank(1, [128, 512])
        F2 = bank(2, [128, 512])
        B3 = bank(3, [128, 512])
        B4 = bank(4, [128, 512])
        B5 = bank(5, [128, 512])
        B6 = bank(6, [128, 512])
        B7 = bank(7, [128, 512])
        B6h = bank(6, [128, 1024], BF16)
        B7h = bank(7, [128, 1024], BF16)

        cs = SB.alloc([128, C_N], F32)
        identb = SB.alloc([128, 128], BF16)
        onesb = SB.alloc([128, 128], BF16)
        onesf = SB.alloc([64, 128], F32)
        epsT = SB.alloc([128, 1], F32)
        gfinT = SB.alloc([128, 16], F32)
        lp = SB.alloc([128, LP_N], F32)
        nA = SB.alloc([64, 16], F32)
        xT = SB.alloc([128, KC, T], F32)
        hT = SB.alloc([128, KC, T], BF16)
        Sst = SB.alloc([128, 16, 128], F32)
        Sbf = SB.alloc([128, 16, 128], BF16)
        halo = SB.alloc([128, NL, 32, 3], F32)
        fhalo = SB.alloc([128, NL, 88, 2], F32)
        oTg = SB.alloc([128, 16, T], BF16)
        wbuf = [SB.alloc([128, 8, 512], BF16) for _ in range(cfg.nb)]
        r1 = SB.off
        qkT = SB.alloc([128, 8, T], BF16)
        vT = SB.alloc([128, 8, T], BF16)
        scrA = SB.off
        SB.off += 16384
        actb = SB.alloc([128, 22, T], BF16, at=r1)
        r2 = SB.off
        usg = SB.alloc([128, 8, T], BF16)
        vg = SB.alloc([128, 8, T], BF16)
        SB.off += 8192
        merged = SB.alloc([128, 16, T], BF16, at=r2 + 8192)
        sqb = [SB.alloc([128, T], BF16) for _ in range(2)]
        cb = [SB.alloc([128, T + 4], F32) for _ in range(2)]
        acc = [SB.alloc([128, T], F32) for _ in range(2)]
        rs = SB.alloc([128, T], F32)
        c_eb = SB.alloc([64, 8, 16], F32)
        c_beta = SB.alloc([64, 8, 16], F32)
        c_negb = SB.alloc([64, 8, 16], F32)
        c_g = SB.alloc([64, 8, 16], F32)
        c_gam = SB.alloc([64, 8, 16], F32)
        c_e2 = SB.alloc([64, 8, 16], F32)
        c_be1 = SB.alloc([64, 8, 16], F32)
        c_tmp = SB.alloc([64, 8, 16], F32)
        c_dec = SB.alloc([128, 8, 16], F32)
        c_ghi = SB.alloc([64, 8, 16], BF16)
        c_glo = SB.alloc([64, 8, 16], BF16)
        cb16 = SB.alloc([64, 256], BF16)

        class Carve:
            def __init__(self, regions):
                self.regions = [[a, b] for a, b in regions]

            def alloc(self, shape, dt):
                esz = 2 if dt == BF16 else 4
                n = esz
                for d in shape[1:]:
                    n *= d
                n = (n + 63) // 64 * 64
                for r in self.regions:
                    if r[1] - r[0] >= n:
                        v = SB.alloc(shape, dt, at=r[0])
                        r[0] += n
                        return v
                raise AssertionError("carve overflow")

        cv = Carve([(scrA, scrA + 16384), (r2, r2 + 24576)])
        d_gtri = cv.alloc([64, 8, 64], BF16)
        d_gtrl = cv.alloc([64, 8, 64], BF16)
        d_Em = cv.alloc([64, 8, 64], F32)
        d_EmT = cv.alloc([64, 8, 64], F32)
        d_E1 = cv.alloc([128, 8, 64], F32)
        d_t1 = cv.alloc([64, 8, 64], F32)
        d_A = [cv.alloc([64, 8, 64], BF16) for _ in range(2)]
        d_B = [cv.alloc([64, 8, 64], BF16) for _ in range(2)]
        d_P = [cv.alloc([64, 8, 64], BF16) for _ in range(2)]
        d_kbg = cv.alloc([64, 8, 128], BF16)
        d_kdec = cv.alloc([64, 8, 128], BF16)
        d_bv = cv.alloc([64, 8, 128], BF16)
        d_attnT = cv.alloc([64, 8, 64], BF16)
        d_qdT = cv.alloc([128, 8, 64], BF16)
        d_wT = cv.alloc([128, 8, 64], BF16)
        d_u = cv.alloc([64, 8, 128], F32)
        d_vn = cv.alloc([64, 8, 128], BF16)
        d_on = cv.alloc([128, 8, 64], F32)
        d_sqo = cv.alloc([128, 8, 64], BF16)
        d_rr = cv.alloc([128, 8, 64], F32)
        d_kq = cv.alloc([64, 512], F32)
        d_sn = SB.alloc([128, 128], F32)
        cv3 = Carve([(scrA, scrA + 16384)])
        brp = cv3.alloc([128, 8, 128], F32)
        wsm = cv3.alloc([128, 8, 128], BF16)
        vtok = cv3.alloc([128, 4, 128], BF16)
        cv4 = Carve([(r2, r2 + 24576)])
        sg = cv4.alloc([128, T], F32)
        pT = cv4.alloc([128, 2, T], BF16)
        cv2 = Carve([(scrA, scrA + 16384)])
        m_sga = cv2.alloc([128, 4, T], BF16)
        m_sgb = cv2.alloc([128, 4, T], BF16)
        m_mt = cv2.alloc([128, 4, T], F32)

        print('SBUF arena used', SB.off, 'of', SB.cap, flush=True)
        def act_op(out, in_, func, reads=None, bias=None, scale=None):
            kw = {}
            rd = [in_] if reads is None else list(reads)
            if bias is not None:
                kw["bias"] = bias.ap if isinstance(bias, V) else bias
                if isinstance(bias, V):
                    rd.append(bias)
            if scale is not None:
                kw["scale"] = scale.ap if isinstance(scale, V) else scale
                if isinstance(scale, V):
                    rd.append(scale)
            s.op("act", lambda e: e.activation(out=out.ap, in_=in_.ap, func=func, **kw), reads=rd, writes=[out])

        def dve_tt(out, a, b, op, aap=None, bap=None, oap=None, eng="dve"):
            s.op(eng, lambda e: e.tensor_tensor(out=oap if oap is not None else out.ap,
                                                in0=aap if aap is not None else a.ap,
                                                in1=bap if bap is not None else b.ap, op=op),
                 reads=[a, b], writes=[out])

        def dve_stt(out, a, scalar, b, op0, op1, oap=None, aap=None, bap=None):
            rd = [a, b]
            sc = scalar
            if isinstance(scalar, V):
                rd.append(scalar)
                sc = scalar.ap
            s.op("dve", lambda e: e.scalar_tensor_tensor(out=oap if oap is not None else out.ap,
                                                         in0=aap if aap is not None else a.ap, scalar=sc,
                                                         in1=bap if bap is not None else b.ap, op0=op0, op1=op1),
                 reads=rd, writes=[out])

        def dve_ts(out, a, scalar, op):
            rd = [a]
            sc = scalar
            if isinstance(scalar, V):
                rd.append(scalar)
                sc = scalar.ap
            s.op("dve", lambda e: e.tensor_single_scalar(out=out.ap, in_=a.ap, scalar=sc, op=op),
                 reads=rd, writes=[out])

        def dve_copy(out, a):
            s.op("dve", lambda e: e.tensor_copy(out=out.ap, in_=a.ap), reads=[a], writes=[out])

        def dve_recip(out, a):
            s.op("dve", lambda e: e.reciprocal(out=out.ap, in_=a.ap), reads=[a], writes=[out])

        def mm(out, pairs, start=True, stop=True):
            rd = []
            for a, b in pairs:
                rd.append(a)
                rd.append(b)
            n = len(pairs)

            def fn(e):
                ins = None
                for i, (a, b) in enumerate(pairs):
                    ins = e.matmul(out.ap, lhsT=a.ap, rhs=b.ap, start=(start and i == 0),
                                   stop=(stop and i == n - 1))
                return ins
            s.op("pe", fn, reads=rd, writes=[out])

        def tr(out, in_, ident):
            s.op("pe", lambda e: e.transpose(out=out.ap, in_=in_.ap, identity=ident.ap),
                 reads=[in_, ident], writes=[out])

        class WStream:
            def __init__(self):
                self.descs = []
                self.pos = 0
                self.issued = 0
                self.released = 0

            def reset(self):
                self.pos = 0
                self.issued = 0
                self.released = 0

            def _pump(self):
                while self.issued < min(len(self.descs), self.released + cfg.nb):
                    self._issue(self.issued)
                    self.issued += 1

            def release(self, n=1):
                if s.plan:
                    return
                self.released += n
                self._pump()

            def get(self, srcs, kc):
                if not isinstance(srcs, (list, tuple)):
                    srcs = [srcs]
                cols = sum(a.shape[1] for a in srcs)
                i = self.pos
                self.pos += 1
                slot = wbuf[i % cfg.nb]
                if s.plan:
                    self.descs.append((srcs, kc))
                    return slot[:, 0:kc, 0:cols]
                self._pump()
                assert i < self.issued, "too many live weight blocks"
                return slot[:, 0:kc, 0:cols]

            def prepare(self):
                assert len(self.descs) % cfg.ntile == 0
                self.npt = len(self.descs) // cfg.ntile
                CH = 128
                self.wq = []
                for c0 in range(0, self.npt, CH):
                    n = min(CH, self.npt - c0)
                    t = nc.dram_tensor("wq%d" % (c0 // CH), [n, 128, 4096], BF16, kind="Internal").ap()
                    for j in range(n):
                        self.wq.append(t[j])
                for b in range(self.npt):
                    srcs, kc = self.descs[b]
                    cols = sum(a.shape[1] for a in srcs)
                    dstb = self.wq[b][:, 0:kc * cols].rearrange("p (k c) -> p k c", k=kc)
                    c0 = 0
                    for src in srcs:
                        ci = src.shape[1]
                        sv = src.rearrange("(kc p) c -> p kc c", p=128)
                        dv = dstb[:, :, c0:c0 + ci]
                        s.dma("pool", lambda e, dv=dv, sv=sv: e.dma_start(out=dv, in_=sv), writes=[("wq", b)])
                        c0 += ci

            def _issue(self, i):
                srcs, kc = self.descs[i]
                cols = sum(a.shape[1] for a in srcs)
                b = i % self.npt
                dst = wbuf[i % cfg.nb][:, 0:kc, 0:cols]
                sv = self.wq[b][:, 0:kc * cols].rearrange("p (k c) -> p k c", k=kc)
                s.dma("sp", lambda e: e.dma_start(out=dst.ap, in_=sv), reads=[("wq", b)], writes=[dst])

        W = WStream()

        def wget2(mat, c0, cols):
            a = W.get(mat[0:1024, c0:c0 + cols], 8)
            b = W.get(mat[1024:2048, c0:c0 + cols], 8)
            return a, b

        def dense16(out_ps, wa, wb, oc, rhsT):
            pairs = []
            for kc in range(8):
                pairs.append((wa[:, kc, oc * 128:(oc + 1) * 128], rhsT[:, kc, :]))
            for kc in range(8):
                pairs.append((wb[:, kc, oc * 128:(oc + 1) * 128], rhsT[:, 8 + kc, :]))
            mm(out_ps, pairs)

        def rmsnorm_h(gain_col0):
            for kc in range(KC):
                sq = sqb[kc % 2]
                act_op(sq, xT[:, kc, :], AF.Square)
                mm(F2, [(onesb, sq)], start=(kc == 0), stop=(kc == KC - 1))
            act_op(rs, F2, AF.Sqrt, bias=epsT, scale=1.0 / D)
            dve_recip(rs, rs)
            for kc in range(KC):
                dve_stt(hT[:, kc, :], xT[:, kc, :], lp[:, gain_col0 + kc:gain_col0 + kc + 1], rs, ALU.mult, ALU.mult)

        def setup():
            s.dma("sp", lambda e: e.dma_start(out=cs.ap, in_=cst), writes=[cs])
            s.dma("sp", lambda e: e.dma_start(out=gfinT.ap, in_=gfin), writes=[gfinT])
            dve_copy(identb, cs[:, C_IDENT:C_IDENT + 128])
            dve_copy(cb16, cs[0:64, C_TRI:C_TRI + 256])
            s.op("dve", lambda e: e.memset(onesb.ap, 1.0), writes=[onesb])
            s.op("dve", lambda e: e.memset(onesf.ap, 1.0), writes=[onesf])
            s.op("dve", lambda e: e.memset(epsT.ap, EPS), writes=[epsT])
            s.op("dve", lambda e: e.memset(Sst.ap, 0.0), writes=[Sst])
            for l in range(NL):
                s.dma("sp", lambda e, l=l: e.dma_start(out=st_d[l], in_=Sst.ap.rearrange("p h d -> p (h d)")),
                      reads=[Sst], writes=[("st", l)])
            s.op("dve", lambda e: e.memset(halo.ap, 0.0), writes=[halo])
            s.op("dve", lambda e: e.memset(fhalo.ap, 0.0), writes=[fhalo])

        def load_layer_params(l):
            s.dma("sp", lambda e: e.dma_start(out=lp.ap, in_=lpar[l]), writes=[lp])
            if cfg.do_a:
                act_op(nA, lp[0:64, LP_ALOG:LP_ALOG + 16], AF.Exp)
                dve_ts(nA, nA, -1.0, ALU.mult)

        tri64 = cb16[:, 0:64]
        U64 = cb16[:, 64:128]
        nms = cb16[:, 128:192]
        nmi = cb16[:, 192:256]
        identb64 = identb[0:64, 0:64]

        def chunk_scalars(l, c):
            wa_, wb_ = W_ba
            pairs = []
            for kc in range(8):
                pairs.append((hT[:, kc, c * 64:(c + 1) * 64], wa_[:, kc, :]))
            for kc in range(8):
                pairs.append((hT[:, 8 + kc, c * 64:(c + 1) * 64], wb_[:, kc, :]))
            ps = F2[0:64, 0:32]
            mm(ps, pairs)
            eb, beta, negb = c_eb[:, c, :], c_beta[:, c, :], c_negb[:, c, :]
            g, gam, e2, be1, tmp = c_g[:, c, :], c_gam[:, c, :], c_e2[:, c, :], c_be1[:, c, :], c_tmp[:, c, :]
            act_op(eb, ps[:, 0:16], AF.Exp, scale=-1.0)
            dve_tt(tmp, ps[:, 16:32], lp[0:64, LP_DTB:LP_DTB + 16], ALU.add)
            dve_ts(eb, eb, 1.0, ALU.add)
            dve_recip(beta, eb)
            dve_ts(negb, beta, -1.0, ALU.mult)
            act_op(tmp, tmp, AF.Exp)
            act_op(tmp, tmp, AF.Ln, bias=1.0)
            dve_tt(g, tmp, nA, ALU.mult)
            ghi, glo = c_ghi[:, c, :], c_glo[:, c, :]
            dve_copy(ghi, g)
            dve_tt(tmp, g, ghi, ALU.subtract)
            dve_copy(glo, tmp)
            pg = F2[0:64, 64:80]
            pl = F2[0:64, 96:112]
            pd = F2[0:128, 128:144]
            mm(pg, [(tri64, ghi), (tri64, glo)])
            mm(pl, [(onesb[0:64, 0:64], ghi), (onesb[0:64, 0:64], glo)])
            mm(pd, [(onesb[0:64, :], ghi), (onesb[0:64, :], glo)])
            act_op(be1, pg, AF.Exp)
            act_op(gam, pg, AF.Copy)
            dve_tt(tmp, pl, gam, ALU.subtract)
            act_op(e2, tmp, AF.Exp)
            dve_tt(be1, be1, beta, ALU.mult)
            act_op(c_dec[:, c, :], pd, AF.Exp)

        def bc_heads(v):
            return v.ap.unsqueeze(1).to_broadcast([v.shape[0], 8, v.shape[1]])

        def delta_chunk(l, c, hg):
            h0 = hg * 8
            j0 = hg * 4
            tsl = slice(c * 64, (c + 1) * 64)
            for jj in range(4):
                tr(B7h[0:64, jj * 128:(jj + 1) * 128], qkT[:, 4 + jj, tsl], identb)
            ktok = B7h[0:64, 0:512]
            kin = ktok.ap.rearrange("p (j d) -> p j d", j=4).unsqueeze(2).to_broadcast([64, 4, 2, 128])

            def sc_bc(v):
                return v.ap.rearrange("p (j r) -> p j r", r=2).unsqueeze(3).to_broadcast([64, 4, 2, 128])
            dve_tt(d_kbg, ktok, c_be1[:, c, h0:h0 + 8], ALU.mult, aap=kin, bap=sc_bc(c_be1[:, c, h0:h0 + 8]),
                   oap=d_kbg.ap.rearrange("p (j r) d -> p j r d", r=2))
            dve_tt(d_kdec, ktok, c_e2[:, c, h0:h0 + 8], ALU.mult, aap=kin, bap=sc_bc(c_e2[:, c, h0:h0 + 8]),
                   oap=d_kdec.ap.rearrange("p (j r) d -> p j r d", r=2))
            for hh in range(8):
                tr(B6h[0:64, hh * 128:(hh + 1) * 128], vT[:, hh, tsl], identb)
            vtk = B6h[0:64, 0:1024]
            dve_tt(d_bv, vtk, c_beta[:, c, h0:h0 + 8], ALU.mult,
                   aap=vtk.ap.rearrange("p (h d) -> p h d", h=8),
                   bap=c_beta[:, c, h0:h0 + 8].ap.unsqueeze(2).to_broadcast([64, 8, 128]))
            if cfg.dstage <= 1:
                return
            for gsrc, gdst in ((c_ghi, d_gtri), (c_glo, d_gtrl)):
                gsl = gsrc[:, c, h0:h0 + 8]
                dve_tt(gdst, tri64, gsl, ALU.mult, aap=bc_heads(tri64),
                       bap=gsl.ap.unsqueeze(2).to_broadcast([64, 8, 64]))
            ones64 = onesb[0:64, :]
            for hh in range(8):
                hs = slice(hh * 64, (hh + 1) * 64)
                gh, gl = d_gtri[:, hh, :], d_gtrl[:, hh, :]
                mm(B3[0:64, hs], [(gh, U64), (gl, U64), (identb64, nms)])
                mm(B4[0:64, hs], [(U64, gh), (U64, gl), (identb64, nmi)])
                mm(B5[0:128, hs], [(ones64, gh), (ones64, gl)])
            act_op(d_Em.flat(), B3[0:64, :], AF.Exp)
            act_op(d_EmT.flat(), B4[0:64, :], AF.Exp)
            act_op(d_E1.flat(), B5, AF.Exp)
            if cfg.dstage <= 2:
                return
            for jj in range(4):
                kTc = qkT[:, 4 + jj, tsl]
                qTc = qkT[:, jj, tsl]
                mm(B6[0:64, jj * 64:(jj + 1) * 64], [(kTc, kTc)])
                mm(B6[0:64, 256 + jj * 64:256 + (jj + 1) * 64], [(kTc, qTc)])
            nb_ = c_negb[:, c, h0:h0 + 8]
            dve_tt(d_t1, d_Em, nb_, ALU.mult, bap=nb_.ap.unsqueeze(2).to_broadcast([64, 8, 64]))
            kk = B6[0:64, 0:256]
            qk = B6[0:64, 256:512]

            def pair_bc(v):
                return v.ap.rearrange("p (j s) -> p j s", j=4).unsqueeze(2).to_broadcast([64, 4, 2, 64])

            def as4(v):
                return v.ap.rearrange("p (j r) s -> p j r s", r=2)
            act_op(d_kq, B6[0:64, :], AF.Copy)
            for jj in range(4):
                kkj = d_kq[:, jj * 64:(jj + 1) * 64]
                qkj = d_kq[:, 256 + jj * 64:256 + (jj + 1) * 64]
                hp = slice(2 * jj, 2 * jj + 2)
                dve_tt(d_A[0][:, hp, :], d_t1[:, hp, :], kkj, ALU.mult,
                       bap=kkj.ap.unsqueeze(1).to_broadcast([64, 2, 64]))
                dve_tt(d_attnT[:, hp, :], d_EmT[:, hp, :], qkj, ALU.mult,
                       bap=qkj.ap.unsqueeze(1).to_broadcast([64, 2, 64]))
                qj = qkT[:, jj, tsl]
                dve_tt(d_qdT[:, hp, :], d_E1[:, hp, :], qj, ALU.mult,
                       bap=qj.ap.unsqueeze(1).to_broadcast([128, 2, 64]))
            if cfg.dstage <= 3:
                return
            for hh in range(8):
                tr(B7h[0:64, 512 + hh * 64:512 + (hh + 1) * 64], d_A[0][:, hh, :], identb64)
            ntp = B7h[0:64, 512:1024]
            act_op(d_B[0].flat(), ntp, AF.Copy)
            dve_tt(d_P[0], d_B[0], identb64, ALU.add, bap=bc_heads(identb64))
            for j in range(5):
                a_, b_ = d_A[j % 2], d_B[j % 2]
                an, bn = d_A[(j + 1) % 2], d_B[(j + 1) % 2]
                for hh in range(8):
                    hs = slice(hh * 64, (hh + 1) * 64)
                    mm(B3[0:64, hs], [(b_[:, hh, :], a_[:, hh, :])])
                if j < 4:
                    for hh in range(8):
                        hs = slice(hh * 64, (hh + 1) * 64)
                        mm(B4[0:64, hs], [(a_[:, hh, :], b_[:, hh, :])])
                act_op(an.flat(), B3[0:64, :], AF.Copy)
                if j < 4:
                    dve_copy(bn.flat(), B4[0:64, :])
                pj, pn = d_P[j % 2], d_P[(j + 1) % 2]
                for hh in range(8):
                    hs = slice(hh * 64, (hh + 1) * 64)
                    mm(B5[0:64, hs], [(an[:, hh, :], pj[:, hh, :])])
                dve_tt(pn.flat(), pj.flat(), B5[0:64, :], ALU.add)
            TT = d_P[1]
            if cfg.dstage <= 4:
                return
            for hh in range(8):
                mm(B5[0:128, hh * 64:(hh + 1) * 64], [(d_kbg[:, hh, :], TT[:, hh, :])])
            act_op(d_wT.flat(), B5, AF.Copy)
            for sb_ in range(2):
                for q in range(4):
                    hh = sb_ * 4 + q
                    mm(B6[0:64, q * 128:(q + 1) * 128], [(TT[:, hh, :], d_bv[:, hh, :])])
                act_op(d_u[:, sb_ * 4:(sb_ + 1) * 4, :].flat(), B6[0:64, :], AF.Copy)
            if cfg.dstage <= 5:
                return
            for sb_ in range(2):
                for q in range(4):
                    hh = sb_ * 4 + q
                    mm(B6[0:64, q * 128:(q + 1) * 128], [(d_wT[:, hh, :], Sbf[:, h0 + hh, :])])
                vnb = d_vn[:, sb_ * 4:(sb_ + 1) * 4, :]
                act_op(d_kq, B6[0:64, :], AF.Copy)
                dve_tt(vnb.flat(), d_u[:, sb_ * 4:(sb_ + 1) * 4, :].flat(), d_kq, ALU.subtract)
                if cfg.dstage == 7:
                    continue
                for q in range(4):
                    hh = sb_ * 4 + q
                    mm(B5[0:128, hh * 64:(hh + 1) * 64], [(Sbf[:, h0 + hh, :], d_qdT[:, hh, :])])
                    mm(B3[0:128, hh * 64:(hh + 1) * 64], [(d_vn[:, hh, :], d_attnT[:, hh, :])])
                for q in range(4):
                    hh = sb_ * 4 + q
                    mm(B7[0:128, q * 128:(q + 1) * 128], [(d_kdec[:, hh, :], d_vn[:, hh, :])])
                if cfg.dstage == 8:
                    continue
                for q in range(4):
                    hh = sb_ * 4 + q
                    sv = Sst[:, h0 + hh, :]
                    act_op(d_sn, B7[0:128, q * 128:(q + 1) * 128], AF.Copy)
                    dcol = c_dec[:, c, h0 + hh:h0 + hh + 1]
                    dve_tt(sv, sv, dcol, ALU.mult, bap=dcol.ap.to_broadcast([128, 128]))
                    dve_tt(sv, sv, d_sn, ALU.add)
                    act_op(Sbf[:, h0 + hh, :], sv, AF.Copy)
            if cfg.dstage <= 8:
                return
            act_op(d_on.flat(), B5, AF.Copy)
            dve_tt(d_on.flat(), B3, d_on.flat(), ALU.add)
            act_op(d_sqo.flat(), d_on.flat(), AF.Square)
            mm(F2, [(onesb, d_sqo.flat())])
            act_op(d_rr.flat(), F2, AF.Sqrt, bias=epsT, scale=1.0 / 128)
            dve_recip(d_rr.flat(), d_rr.flat())
            ov = oTg[:, h0:h0 + 8, tsl]
            dve_stt(ov, d_on, lp[:, LP_HG:LP_HG + 1], d_rr, ALU.mult, ALU.mult)

        W_ba = [None, None]

        def mixer_a(l):
            Wl = w_in[l]
            s.dma("sp", lambda e: e.dma_start(out=Sst.ap.rearrange("p h d -> p (h d)"), in_=st_d[l]),
                  reads=[("st", l)], writes=[Sst])
            W_ba[0] = W.get(Wl[0:1024, 6144:6176], 8)
            W_ba[1] = W.get(Wl[1024:2048, 6144:6176], 8)
            for c in range(8):
                if cfg.dstage >= -1:
                    chunk_scalars(l, c)
            W.release(2)
            if cfg.dstage <= -1:
                return
            for h in range(16):
                act_op(Sbf[:, h, :], Sst[:, h, :], AF.Copy)
            for hg in range(2):
                blocks = [(hg * 512, 0), (1024 + hg * 512, 1), (2048 + hg * 1024, 2), (2048 + hg * 1024 + 512, 3)]
                for c0, kind in blocks:
                    wa_, wb_ = wget2(Wl, c0, 512)
                    for oc in range(4):
                        ci = c0 // 128 + oc
                        ps = F0 if oc % 2 == 0 else F1
                        dense16(ps, wa_, wb_, oc, hT)
                        cbv = cb[oc % 2]
                        ac = acc[oc % 2]
                        act_op(cbv[:, 3:3 + T], ps, AF.Copy)
                        hv = halo[:, l, ci, :]
                        act_op(cbv[:, 0:3], hv, AF.Copy)
                        act_op(hv, cbv[:, T:T + 3], AF.Copy)
                        cw0 = LP_CONV + ci * 4
                        dve_ts(ac, cbv[:, 0:T], lp[:, cw0:cw0 + 1], ALU.mult)
                        for j in range(1, 4):
                            dve_stt(ac, cbv[:, j:j + T], lp[:, cw0 + j:cw0 + j + 1], ac, ALU.mult, ALU.add)
                        if kind < 2:
                            act_op(ac, ac, AF.Silu)
                            sq = sqb[oc % 2]
                            act_op(sq, ac, AF.Square)
                            mm(F2, [(onesb, sq)])
                            act_op(rs, F2, AF.Sqrt, bias=epsT)
                            dve_recip(rs, rs)
                            if kind == 0:
                                dve_stt(qkT[:, oc, :], ac, 128.0 ** -0.5, rs, ALU.mult, ALU.mult)
                            else:
                                dve_tt(qkT[:, 4 + oc, :], ac, rs, ALU.mult)
                        else:
                            act_op(vT[:, (kind - 2) * 4 + oc, :], ac, AF.Silu)
                    W.release(2)
                for c in range(8):
                    if cfg.dstage >= 1:
                        delta_chunk(l, c, hg)
            s.dma("sp", lambda e: e.dma_start(out=st_d[l], in_=Sst.ap.rearrange("p h d -> p (h d)")),
                  reads=[Sst], writes=[("st", l)])
            for blk in range(4):
                wa_, wb_ = wget2(Wl, 4096 + blk * 512, 512)
                for oc in range(4):
                    ci = blk * 4 + oc
                    ps = F0 if ci % 2 == 0 else F1
                    dense16(ps, wa_, wb_, oc, hT)
                    ac = acc[ci % 2]
                    act_op(ac, ps, AF.Silu)
                    dve_tt(oTg[:, ci, :], oTg[:, ci, :], ac, ALU.mult)
                W.release(2)

        def mixer_b(l):
            Wl = w_in[l]
            for half in range(2):
                stg = acc[half]
                s.dma("sp", lambda e, half=half, stg=stg: e.dma_start(
                    out=stg.ap, in_=wsT[l, :, half * 512:(half + 1) * 512]), writes=[stg])
                for gg in range(4):
                    g = half * 4 + gg
                    dve_tt(wsm[:, g, :], stg[:, gg * 128:(gg + 1) * 128], cs[:, C_SGM:C_SGM + 128], ALU.mult)
            s.dma("sp", lambda e: e.dma_start(out=brp.ap.rearrange("p g t -> p (g t)"), in_=brep[l]),
                  writes=[brp])
            for blk in range(2):
                wa_, wb_ = wget2(Wl, 6176 + blk * 512, 512)
                for oc in range(4):
                    ci = blk * 4 + oc
                    ps = F0 if ci % 2 == 0 else F1
                    dense16(ps, wa_, wb_, oc, hT)
                    act_op(usg[:, ci, :], ps, AF.Gelu_apprx_tanh)
                W.release(2)
            for blk in range(2):
                wa_, wb_ = wget2(Wl, 7200 + blk * 512, 512)
                for oc in range(4):
                    ci = blk * 4 + oc
                    ps = F0 if ci % 2 == 0 else F1
                    dense16(ps, wa_, wb_, oc, hT)
                    ac = acc[ci % 2]
                    act_op(ac, ps, AF.Gelu_apprx_tanh)
                    sq = sqb[ci % 2]
                    act_op(sq, ac, AF.Square)
                    mm(F2, [(onesb, sq)], start=(ci == 0), stop=(ci == 7))
                    dve_copy(vg[:, ci, :], ac)
                W.release(2)
            act_op(rs, F2, AF.Sqrt, bias=epsT, scale=1.0 / 1024)
            dve_recip(rs, rs)
            for ci in range(8):
                dve_stt(vg[:, ci, :], vg[:, ci, :], lp[:, LP_SGUG + ci:LP_SGUG + ci + 1], rs, ALU.mult, ALU.mult)
            for g in range(8):
                for tc in range(4):
                    tr(B7h[0:128, tc * 128:(tc + 1) * 128], vg[:, g, tc * 128:(tc + 1) * 128], identb)
                act_op(vtok.flat(), B7h[0:128, 0:512], AF.Copy)
                for tc in range(4):
                    mm(B5[0:128, tc * 128:(tc + 1) * 128], [(vtok[:, tc, :], wsm[:, g, :])])
                mx = acc[g % 2]
                dve_tt(mx, B5, brp[:, g, :], ALU.add, aap=B5.ap.rearrange("p (c t) -> p c t", c=4),
                       bap=brp[:, g, :].ap.unsqueeze(1).to_broadcast([128, 4, 128]),
                       oap=mx.ap.rearrange("p (c t) -> p c t", c=4))
                dve_tt(usg[:, g, :], usg[:, g, :], mx, ALU.mult)

        def merge_out(l):
            Wl = w_in[l]
            GA = 8224
            GB = 8224 + 2048
            for blk in range(4):
                wa_, wb_ = wget2(Wl, GA + blk * 512, 512)
                for oc in range(4):
                    ps = F0 if oc % 2 == 0 else F1
                    dense16(ps, wa_, wb_, oc, hT)
                    act_op(m_sga[:, oc, :], ps, AF.Sigmoid)
                W.release(2)
                if cfg.do_a:
                    wa_, wb_ = wget2(w_a[l], blk * 512, 512)
                    for oc in range(4):
                        ps = F0 if oc % 2 == 0 else F1
                        dense16(ps, wa_, wb_, oc, oTg)
                        dve_tt(m_mt[:, oc, :], ps, m_sga[:, oc, :], ALU.mult)
                    W.release(2)
                else:
                    s.op("dve", lambda e: e.memset(m_mt.ap, 0.0), writes=[m_mt])
                wa_, wb_ = wget2(Wl, GB + blk * 512, 512)
                for oc in range(4):
                    ps = F0 if oc % 2 == 0 else F1
                    dense16(ps, wa_, wb_, oc, hT)
                    act_op(m_sgb[:, oc, :], ps, AF.Sigmoid)
                W.release(2)
                if cfg.do_b:
                    wb8 = W.get(w_b[l][:, blk * 512:(blk + 1) * 512], 8)
                    for oc in range(4):
                        ps = F0 if oc % 2 == 0 else F1
                        mm(ps, [(wb8[:, kc, oc * 128:(oc + 1) * 128], usg[:, kc, :]) for kc in range(8)])
                        ac = acc[oc % 2]
                        dve_tt(ac, ps, m_sgb[:, oc, :], ALU.mult)
                        dve_tt(merged[:, blk * 4 + oc, :], m_mt[:, oc, :], ac, ALU.add)
                    W.release(1)
                else:
                    for oc in range(4):
                        dve_copy(merged[:, blk * 4 + oc, :], m_mt[:, oc, :])
            for blk in range(4):
                wa_, wb_ = wget2(w_o[l], blk * 512, 512)
                for oc in range(4):
                    j = blk * 4 + oc
                    ps = F0 if oc % 2 == 0 else F1
                    dense16(ps, wa_, wb_, oc, merged)
                    dve_tt(xT[:, j, :], xT[:, j, :], ps, ALU.add)
                W.release(2)

        def ffn(l):
            rmsnorm_h(LP_GFFN)
            Wu = w_up[l]
            Wd = w_dn[l]
            banks = [F0, F1, F2, B3]
            for half in range(2):
                for blk in range(11):
                    c0 = half * 2816 + blk * 256
                    ha = W.get([Wu[0:1024, c0:c0 + 256], Wu[0:1024, DFF + c0:DFF + c0 + 256]], 8)
                    hb = W.get([Wu[1024:2048, c0:c0 + 256], Wu[1024:2048, DFF + c0:DFF + c0 + 256]], 8)
                    for oc in range(2):
                        jl = blk * 2 + oc
                        jj = half * 22 + jl
                        dense16(F0, ha, hb, oc, hT)
                        dense16(F1, ha, hb, 2 + oc, hT)
                        accs = []
                        for which, ps in ((0, F0), (1, F1)):
                            ch = jj + which * 44
                            cbv = cb[which]
                            ac = acc[which]
                            act_op(cbv[:, 2:2 + T], ps, AF.Copy)
                            hv = fhalo[:, l, ch, :]
                            act_op(cbv[:, 0:2], hv, AF.Copy)
                            act_op(hv, cbv[:, T:T + 2], AF.Copy)
                            cw0 = LP_CFFN + ch * 3
                            dve_ts(ac, cbv[:, 0:T], lp[:, cw0:cw0 + 1], ALU.mult)
                            for j in range(1, 3):
                                dve_stt(ac, cbv[:, j:j + T], lp[:, cw0 + j:cw0 + j + 1], ac, ALU.mult, ALU.add)
                            accs.append(ac)
                        act_op(sg, accs[0], AF.Silu, bias=lp[:, LP_BFFN + jj:LP_BFFN + jj + 1])
                        dve_stt(actb[:, jl, :], accs[1], lp[:, LP_BFFN + 44 + jj:LP_BFFN + 44 + jj + 1], sg,
                                ALU.add, ALU.mult)
                    W.release(2)
                r0 = half * 2816
                for blk in range(4):
                    for kb in range(3):
                        nk = 8 if kb < 2 else 6
                        wv = W.get(Wd[r0 + kb * 1024:r0 + kb * 1024 + nk * 128, blk * 512:(blk + 1) * 512], nk)
                        for oc in range(4):
                            mm(banks[oc], [(wv[:, kc, oc * 128:(oc + 1) * 128], actb[:, kb * 8 + kc, :])
                                           for kc in range(nk)], start=(kb == 0), stop=(kb == 2))
                        W.release(1)
                    for oc in range(4):
                        j = blk * 4 + oc
                        dve_tt(xT[:, j, :], xT[:, j, :], banks[oc], ALU.add)

        def ple(l, t0):
            rmsnorm_h(LP_GPLE)
            pv = p_t[l].rearrange("(kc p) s -> p kc s", p=128)[:, :, t0:t0 + T]
            s.dma("pool", lambda e: e.dma_start(out=pT.ap, in_=pv), writes=[pT])
            for blk in range(4):
                wa_, wb_ = wget2(w_g[l], blk * 512, 512)
                wp_ = W.get(w_p[l][:, blk * 512:(blk + 1) * 512], 2)
                for oc in range(4):
                    j = blk * 4 + oc
                    dense16(F0, wa_, wb_, oc, hT)
                    mm(F1, [(wp_[:, kc, oc * 128:(oc + 1) * 128], pT[:, kc, :]) for kc in range(2)])
                    act_op(sg, F0, AF.Sigmoid)
                    ac = acc[oc % 2]
                    dve_tt(ac, F1, sg, ALU.mult)
                    dve_tt(xT[:, j, :], xT[:, j, :], ac, ALU.add)
                W.release(3)

        def body():
            W.reset()
            setup()
            xv = x_t.rearrange("(kc p) s -> p kc s", p=128)
            ov = out_t.rearrange("(kc p) s -> p kc s", p=128)
            for ti in range(cfg.ntile):
                t0 = ti * T
                s.dma("sp", lambda e, t0=t0: e.dma_start(out=xT.ap, in_=xv[:, :, t0:t0 + T]), writes=[xT])
                for l in range(NL):
                    load_layer_params(l)
                    if cfg.do_a or cfg.do_b:
                        rmsnorm_h(LP_GMIX)
                        if cfg.do_a:
                            mixer_a(l)
                        if cfg.do_b:
                            mixer_b(l)
                        merge_out(l)
                    if cfg.do_ffn:
                        ffn(l)
                    if cfg.do_ple:
                        ple(l, t0)
                if cfg.final_norm:
                    for kc in range(KC):
                        sq = sqb[kc % 2]
                        act_op(sq, xT[:, kc, :], AF.Square)
                        mm(F2, [(onesb, sq)], start=(kc == 0), stop=(kc == KC - 1))
                    act_op(rs, F2, AF.Sqrt, bias=epsT, scale=1.0 / D)
                    dve_recip(rs, rs)
                for kc in range(KC):
                    if cfg.final_norm:
                        yo = acc[kc % 2]
                        dve_stt(yo, xT[:, kc, :], gfinT[:, kc:kc + 1], rs, ALU.mult, ALU.mult)
                    else:
                        yo = xT[:, kc, :]
                    s.dma("sp", lambda e, kc=kc, yo=yo, t0=t0: e.dma_start(out=ov[:, kc, t0:t0 + T], in_=yo.ap),
                          reads=[yo], writes=["out"])
            s.finish("sp", ["out"])

        s.plan = True
        body()
        s.plan = False
        W.prepare()
        body()
        s.emit()
    return nc, s


def prep_layer_params(inp, layers):
    nl = len(layers)
    lp = np.zeros((nl, 128, LP_N), np.float32)
    for i, l in enumerate(layers):
        lp[i, :, LP_GMIX:LP_GMIX + 16] = inp["norm_mix"][l].reshape(16, 128).T
        lp[i, :, LP_GFFN:LP_GFFN + 16] = inp["norm_ffn"][l].reshape(16, 128).T
        lp[i, :, LP_GPLE:LP_GPLE + 16] = inp["norm_ple"][l].reshape(16, 128).T
        cw = inp["conv_qkv"][l].reshape(4, 32, 128)
        lp[i, :, LP_CONV:LP_CONV + 128] = cw.transpose(2, 1, 0).reshape(128, 128)
        lp[i, :, LP_DTB:LP_DTB + 16] = inp["dt_bias"][l][None, :]
        lp[i, :, LP_ALOG:LP_ALOG + 16] = inp["a_log"][l][None, :]
        lp[i, :, LP_HG] = inp["head_norm"][l]
        lp[i, :, LP_SGUG:LP_SGUG + 8] = inp["sgu_norm"][l].reshape(8, 128).T
        cf = inp["conv_ffn"][l].reshape(3, 88, 128)
        lp[i, :, LP_CFFN:LP_CFFN + 264] = cf.transpose(2, 1, 0).reshape(128, 264)
        lp[i, :, LP_BFFN:LP_BFFN + 88] = inp["b_conv_ffn"][l].reshape(88, 128).T
    return lp


def make_in_map(inp, b, layers, ntile):
    S = ntile * T
    ls = list(layers)
    m = {
        "x_t": np.ascontiguousarray(inp["x"][b, :S].T),
        "p_t": np.ascontiguousarray(inp["p"][ls][:, b, :S].transpose(0, 2, 1)),
        "w_in": np.ascontiguousarray(inp["w_in"][ls]),
        "w_a": np.ascontiguousarray(inp["w_branch_a"][ls]),
        "w_b": np.ascontiguousarray(inp["w_branch_b"][ls]),
        "w_o": np.ascontiguousarray(inp["w_out"][ls]),
        "w_up": np.ascontiguousarray(inp["w_ffn_up"][ls]),
        "w_dn": np.ascontiguousarray(inp["w_ffn_down"][ls]),
        "w_g": np.ascontiguousarray(inp["w_ple_gate"][ls]),
        "w_p": np.ascontiguousarray(inp["w_ple_proj"][ls]),
        "lpar": prep_layer_params(inp, ls),
        "wsT": np.ascontiguousarray(inp["w_spatial"][ls].transpose(0, 3, 1, 2)).reshape(len(ls), 128, 1024),
        "brep": np.ascontiguousarray(np.broadcast_to(inp["b_spatial"][ls].reshape(len(ls), 1, 1024),
                                                     (len(ls), 128, 1024))),
        "gfin": np.ascontiguousarray(inp["norm_final"].reshape(16, 128).T),
        "cst": make_consts(),
    }
    return m


_CACHE = {}


def _get_nc(nl, ntile, final_norm):
    key = (nl, ntile, final_norm)
    if key not in _CACHE:
        cfg = Cfg(nl, ntile)
        cfg.final_norm = final_norm
        _CACHE[key] = build(cfg)[0]
    return _CACHE[key]


def kernel(**inputs):
    inp = {k: np.asarray(v) for k, v in inputs.items()}
    B, S, _ = inp["x"].shape
    depth = inp["w_in"].shape[0]
    ntile = S // T
    nc = _get_nc(depth, ntile, True)
    in_maps = [make_in_map(inp, b, range(depth), ntile) for b in range(B)]
    res = run_bass_kernel_spmd(nc, in_maps, core_ids=list(range(B)))
    out = np.stack([r["out_t"].T for r in res.results], axis=0)
    return np.ascontiguousarray(out.astype(np.float32))
```
